# Optimizing a Trainium2 kernel written in Bass

```python
import jax, jax.numpy as jnp
from jax import lax
import numpy as np

D_MODEL = 1024
BATCH = 32
SEQ = 2048
DEPTH = 2

N_MEM = 256
EPS = 1e-6
FOX_HEAD_DIM = 64
FOX_WIDTH = D_MODEL // 2
FOX_HEADS = FOX_WIDTH // FOX_HEAD_DIM
GMLP_GROUP_DIM = 64
GMLP_WIDTH = D_MODEL // 2
GMLP_GROUPS = GMLP_WIDTH // GMLP_GROUP_DIM
CHUNK = 128
Q_BLOCK = 128
MIX_WIDTH = FOX_WIDTH + GMLP_WIDTH
IN_WIDTH = 3 * FOX_WIDTH + FOX_HEADS + 2 * GMLP_WIDTH
CONV_WIDTH = D_MODEL
CONV_KERNEL = 31
XA_HEADS = 4
XA_HEAD_DIM = D_MODEL // XA_HEADS
FFN_HIDDEN = -(-8 * D_MODEL // (3 * 256)) * 256
N_EVEN = (DEPTH + 1) // 2
N_ODD = DEPTH // 2

kernel_name = "hybrid_gmlp_fox_conformer_memxattn"


def rmsnorm(x, g):
    x32 = x.astype(jnp.float32)
    y = x32 * lax.rsqrt(jnp.mean(x32 * x32, axis=-1, keepdims=True) + EPS)
    return (y * g.astype(jnp.float32)).astype(x.dtype)


def layernorm(x, g, b):
    x32 = x.astype(jnp.float32)
    mu = jnp.mean(x32, axis=-1, keepdims=True)
    xc = x32 - mu
    y = xc * lax.rsqrt(jnp.mean(xc * xc, axis=-1, keepdims=True) + EPS)
    return (y * g.astype(jnp.float32) + b.astype(jnp.float32)).astype(x.dtype)


def fox_attention(q, k, v, f_logit, f_bias):
    B, T, _ = q.shape
    scale = FOX_HEAD_DIM ** -0.5
    q = q.reshape(B, T, FOX_HEADS, FOX_HEAD_DIM) * scale
    k = k.reshape(B, T, FOX_HEADS, FOX_HEAD_DIM)
    v = v.reshape(B, T, FOX_HEADS, FOX_HEAD_DIM)
    log_f = jax.nn.log_sigmoid((f_logit + f_bias).astype(jnp.float32))
    cum = jnp.cumsum(log_f, axis=1).transpose(0, 2, 1)
    outs = []
    for i in range(T // Q_BLOCK):
        q0 = i * Q_BLOCK
        q1 = q0 + Q_BLOCK
        s = jnp.einsum('bqhd,bkhd->bhqk', q[:, q0:q1], k[:, :q1]).astype(jnp.float32)
        s = s + cum[:, :, q0:q1, None] - cum[:, :, None, :q1]
        causal = (q0 + jnp.arange(Q_BLOCK))[:, None] >= jnp.arange(q1)[None, :]
        p = jax.nn.softmax(jnp.where(causal, s, -jnp.inf), axis=-1).astype(v.dtype)
        outs.append(jnp.einsum('bhqk,bkhd->bqhd', p, v[:, :q1]))
    return jnp.concatenate(outs, axis=1).reshape(B, T, FOX_WIDTH)


def gmlp_spatial_gate(z, ln_g, ln_b, w_s, b_s):
    B, T, _ = z.shape
    z = jax.nn.gelu(z)
    u, vg = jnp.split(z, 2, axis=-1)
    vg = layernorm(vg, ln_g, ln_b)
    vg = vg.reshape(B, T // CHUNK, CHUNK, GMLP_GROUPS, GMLP_GROUP_DIM)
    w = w_s * jnp.tril(jnp.ones((CHUNK, CHUNK), dtype=w_s.dtype))
    mixed = jnp.einsum('gts,bcsgd->bctgd', w, vg) + b_s.T[:, :, None]
    return u * mixed.reshape(B, T, GMLP_WIDTH)


def even_mixer(h, w_in, f_bias, ln_g, ln_b, w_s, b_s, w_out):
    proj = h @ w_in
    F = FOX_WIDTH
    q, k, v, f_logit, z = jnp.split(proj, [F, 2 * F, 3 * F, 3 * F + FOX_HEADS], axis=-1)
    a_out = gmlp_spatial_gate(z, ln_g, ln_b, w_s, b_s)
    b_out = fox_attention(q, k, v, f_logit, f_bias)
    return jnp.concatenate([b_out, a_out], axis=-1) @ w_out


def conformer_conv(h, w_in, b_in, dw_w, dw_b, ln_g, ln_b, w_out, b_out):
    a, g = jnp.split(h @ w_in + b_in, 2, axis=-1)
    y = a * jax.nn.sigmoid(g)
    y = lax.conv_general_dilated(
        y, dw_w[:, None, :].astype(y.dtype), window_strides=(1,),
        padding=[(CONV_KERNEL - 1, 0)], dimension_numbers=('NWC', 'WIO', 'NWC'),
        feature_group_count=CONV_WIDTH) + dw_b
    y = jax.nn.silu(layernorm(y, ln_g, ln_b))
    return y @ w_out + b_out


def memory_cross_attention(h, m, wq, wkv, wo):
    B, T, _ = h.shape
    q = (h @ wq).reshape(B, T, XA_HEADS, XA_HEAD_DIM) * (XA_HEAD_DIM ** -0.5)
    k, v = jnp.split(m @ wkv, 2, axis=-1)
    k = k.reshape(B, -1, XA_HEADS, XA_HEAD_DIM)
    v = v.reshape(B, -1, XA_HEADS, XA_HEAD_DIM)
    s = jnp.einsum('bthd,bmhd->bhtm', q, k).astype(jnp.float32)
    p = jax.nn.softmax(s, axis=-1).astype(v.dtype)
    o = jnp.einsum('bhtm,bmhd->bthd', p, v).reshape(B, T, D_MODEL)
    return o @ wo


def swiglu(h, w_gu, w_down):
    g, u = jnp.split(h @ w_gu, 2, axis=-1)
    return (jax.nn.silu(g) * u) @ w_down


def setup_inputs(seed: int = 0) -> dict:
    key = jax.random.key(seed)
    ks = iter(jax.random.split(key, 40))

    def nrm(shape, scale):
        return jax.random.normal(next(ks), shape, jnp.float32) * scale

    def gain(shape):
        return 1.0 + nrm(shape, 0.02)

    D = D_MODEL
    return {
        "x": nrm((BATCH, SEQ, D), 1.0),
        "mem": nrm((BATCH, N_MEM, D), 1.0),
        "mix_norm_e": gain((N_EVEN, D)),
        "w_in_e": nrm((N_EVEN, D, IN_WIDTH), D ** -0.5),
        "fox_f_bias": 2.0 + nrm((N_EVEN, FOX_HEADS), 0.5),
        "gmlp_ln_g": gain((N_EVEN, GMLP_WIDTH)),
        "gmlp_ln_b": nrm((N_EVEN, GMLP_WIDTH), 0.02),
        "gmlp_w_s": nrm((N_EVEN, GMLP_GROUPS, CHUNK, CHUNK), CHUNK ** -0.5),
        "gmlp_b_s": gain((N_EVEN, GMLP_GROUPS, CHUNK)),
        "w_out_e": nrm((N_EVEN, MIX_WIDTH, D), MIX_WIDTH ** -0.5),
        "mix_norm_o": gain((N_ODD, D)),
        "conv_w_in": nrm((N_ODD, D, 2 * CONV_WIDTH), D ** -0.5),
        "conv_b_in": nrm((N_ODD, 2 * CONV_WIDTH), 0.02),
        "conv_dw_w": nrm((N_ODD, CONV_KERNEL, CONV_WIDTH), CONV_KERNEL ** -0.5),
        "conv_dw_b": nrm((N_ODD, CONV_WIDTH), 0.02),
        "conv_ln_g": gain((N_ODD, CONV_WIDTH)),
        "conv_ln_b": nrm((N_ODD, CONV_WIDTH), 0.02),
        "conv_w_out": nrm((N_ODD, CONV_WIDTH, D), CONV_WIDTH ** -0.5),
        "conv_b_out": nrm((N_ODD, D), 0.02),
        "xa_norm": gain((DEPTH, D)),
        "mem_norm": gain((DEPTH, D)),
        "xa_wq": nrm((DEPTH, D, D), D ** -0.5),
        "xa_wkv": nrm((DEPTH, D, 2 * D), D ** -0.5),
        "xa_wo": nrm((DEPTH, D, D), D ** -0.5),
        "ffn_norm": gain((DEPTH, D)),
        "ffn_w_gu": nrm((DEPTH, D, 2 * FFN_HIDDEN), D ** -0.5),
        "ffn_w_down": nrm((DEPTH, FFN_HIDDEN, D), FFN_HIDDEN ** -0.5),
        "final_norm": gain((D,)),
    }


def reference(x, mem, mix_norm_e, w_in_e, fox_f_bias, gmlp_ln_g, gmlp_ln_b, gmlp_w_s,
              gmlp_b_s, w_out_e, mix_norm_o, conv_w_in, conv_b_in, conv_dw_w, conv_dw_b,
              conv_ln_g, conv_ln_b, conv_w_out, conv_b_out, xa_norm, mem_norm, xa_wq,
              xa_wkv, xa_wo, ffn_norm, ffn_w_gu, ffn_w_down, final_norm):
    for layer in range(DEPTH):
        li = layer // 2
        if layer % 2 == 0:
            h = rmsnorm(x, mix_norm_e[li])
            x = x + even_mixer(h, w_in_e[li], fox_f_bias[li], gmlp_ln_g[li], gmlp_ln_b[li],
                               gmlp_w_s[li], gmlp_b_s[li], w_out_e[li])
        else:
            h = rmsnorm(x, mix_norm_o[li])
            x = x + conformer_conv(h, conv_w_in[li], conv_b_in[li], conv_dw_w[li],
                                   conv_dw_b[li], conv_ln_g[li], conv_ln_b[li],
                                   conv_w_out[li], conv_b_out[li])
        h = rmsnorm(x, xa_norm[layer])
        m = rmsnorm(mem, mem_norm[layer])
        x = x + memory_cross_attention(h, m, xa_wq[layer], xa_wkv[layer], xa_wo[layer])
        h = rmsnorm(x, ffn_norm[layer])
        x = x + swiglu(h, ffn_w_gu[layer], ffn_w_down[layer])
    return rmsnorm(x, final_norm)
```

```python
import numpy as np
from contextlib import ExitStack
import concourse.bass as bass
import concourse.mybir as mybir
from concourse.bass_utils import run_bass_kernel_spmd

F32 = mybir.dt.float32
BF16 = mybir.dt.bfloat16
AF = mybir.ActivationFunctionType
ALU = mybir.AluOpType

ENG = ['pe', 'act', 'dve', 'pool', 'sp']
T = 2048
D = 1024
NT = 16
EPS = 1e-6
FFN_H = 2816
NCORES = 8


class Buf:
    __slots__ = ('w', 'r')

    def __init__(self, src=None):
        self.w = dict(src.w) if src is not None else {}
        self.r = dict(src.r) if src is not None else {}


class Prog:
    def __init__(self, nc, es):
        self.nc = nc
        self.tag = ''
        self.tags = {e: [] for e in ENG}
        self.ops = {e: [] for e in ENG}
        self.cnt = {e: 0 for e in ENG}
        self.seen = {e: {} for e in ENG}
        self.sems = {}
        self.dma_cnt = {}
        self.es = es
        for e in ENG:
            if e != 'sp':
                self.sems[e] = es.enter_context(nc.semaphore('s_' + e))
        self.n_dma = 0

    def dma_sem(self):
        k = ('dma', self.n_dma)
        self.n_dma += 1
        self.sems[k] = self.es.enter_context(self.nc.semaphore('d%d' % k[1]))
        self.dma_cnt[k] = 0
        return k

    def _waits(self, e, deps):
        best = {}
        for k, v in deps:
            if e == 'pe' and k == 'pe':
                continue
            if v > best.get(k, 0):
                best[k] = v
        out = []
        for k, v in best.items():
            if v > self.seen[e].get(k, 0):
                self.seen[e][k] = v
                out.append((k, v))
        return out

    @staticmethod
    def _deps(reads, writes):
        deps = []
        for b in reads:
            deps.extend(b.w.items())
        for b in writes:
            deps.extend(b.w.items())
            deps.extend(b.r.items())
        return deps

    @staticmethod
    def _mark(reads, writes, k, v):
        for b in reads:
            if v > b.r.get(k, 0):
                b.r[k] = v
        for b in writes:
            if v > b.w.get(k, 0):
                b.w[k] = v

    def op(self, e, fn, reads=(), writes=()):
        waits = self._waits(e, self._deps(reads, writes))
        self.cnt[e] += 1
        self.tags[e].append((self.tag, tuple(waits)))
        self.ops[e].append((waits, fn, e))
        self._mark(reads, writes, e, self.cnt[e])

    def dma(self, q, semk, out, in_, reads=(), writes=(), **kw):
        waits = self._waits(q, self._deps(reads, writes))
        self.dma_cnt[semk] += 16
        v = self.dma_cnt[semk]
        self.ops[q].append((waits, lambda eng, o=out, i=in_, kw=kw: eng.dma_start(out=o, in_=i, **kw), semk))
        self._mark(reads, writes, semk, v)

    def fence(self, engines_only=False):
        b = Buf()
        for e in ENG:
            if e != 'sp' and self.cnt[e] > 0:
                b.w[e] = self.cnt[e]
        if engines_only:
            return b
        for k, v in self.dma_cnt.items():
            if v > 0:
                b.w[k] = v
        return b

    def final_wait(self, e, bufs):
        deps = []
        for b in bufs:
            deps.extend(b.w.items())
            deps.extend(b.r.items())
        self.ops[e].append((self._waits(e, deps), None, None))

    def emit(self):
        prog = self
        with self.nc.Block() as block:
            def run(ename):
                def body(eng):
                    for waits, fn, inc in prog.ops[ename]:
                        for k, v in waits:
                            eng.wait_ge(prog.sems[k], v)
                        if fn is None:
                            continue
                        ins = fn(eng)
                        if isinstance(inc, tuple):
                            ins.then_inc(prog.sems[inc], 16)
                        else:
                            ins.then_inc(prog.sems[inc], 1)
                return body
            block.tensor(run('pe'))
            block.scalar(run('act'))
            block.vector(run('dve'))
            block.gpsimd(run('pool'))
            block.sync(run('sp'))


class Rot:
    def __init__(self, items):
        self.items = list(items)
        self.i = 0

    def next(self):
        it = self.items[self.i % len(self.items)]
        self.i += 1
        return it


COLS = {}
_c = 0
for _name, _n in [('mix_e', 8), ('xa0', 8), ('ffn0', 8), ('mix_o', 8), ('xa1', 8), ('ffn1', 8),
                  ('mem0', 8), ('mem1', 8), ('cbin', 16), ('dwb', 8), ('clng', 8), ('clnb', 8),
                  ('dww', 248), ('gbs', 8), ('fbias', 1)]:
    COLS[_name] = (_c, _n)
    _c += _n
NCOL = _c


def build_program(NSEQ, stages, final_norm=True):
    SPLIT_LAST = True
    nc = bass.Bass("TRN2", target_bir_lowering=False)

    def din(name, shape, dt=F32):
        return nc.dram_tensor(name, list(shape), dt, kind="ExternalInput").ap()

    x_d = din("x", [NSEQ, T, D])
    mem_d = din("mem", [NSEQ, 256, D])
    out_d = nc.dram_tensor("out", [NSEQ, T, D], F32, kind="ExternalOutput").ap()
    w_in_e = din("w_in_e", [D, 2568])
    w_out_e = din("w_out_e", [D, D])
    conv_w_in = din("conv_w_in", [D, 2048])
    conv_w_out = din("conv_w_out", [D, D])
    xa_wq = din("xa_wq", [2, D, D])
    xa_wkv = din("xa_wkv", [2, D, 2048])
    xa_wo = din("xa_wo", [2, D, D])
    ffn_w_gu = din("ffn_w_gu", [2, D, 2 * FFN_H])
    ffn_w_down = din("ffn_w_down", [2, FFN_H, D])
    cols_d = din("cols", [128, NCOL])
    rows_d = din("rows", [128, 2176])
    bout_d = din("bout", [1, D])
    wsT_d = din("wsT", [128, 8, 128])
    cst_d = din("cst", [128, 4, 128])
    c32_d = din("c32", [128, 3, 128])

    def dscr(name, shape):
        return nc.dram_tensor(name, list(shape), BF16, kind="Internal").ap()

    s_w_in_e = dscr("s_w_in_e", [128, 8 * 2568])
    s_w_out_e = dscr("s_w_out_e", [128, 8 * 1024])
    s_conv_w_in = dscr("s_conv_w_in", [128, 8 * 2048])
    s_conv_w_out = dscr("s_conv_w_out", [128, 8 * 1024])
    cumd = dscr("cumd", [128, 6 * 128])
    s_diag = [dscr("s_diag%d" % cc, [128, 31 * 128]) for cc in range(8)]
    kv_scr = [dscr("kv_scr%d" % l, [128, 4096]) for l in range(2)]
    s_wq = [dscr("s_wq%d" % l, [128, 8 * 1024]) for l in range(2)]
    s_wkv = [dscr("s_wkv%d" % l, [128, 8 * 2048]) for l in range(2)]
    s_wo = [dscr("s_wo%d" % l, [128, 8 * 1024]) for l in range(2)]
    s_wgu = [dscr("s_wgu%d" % l, [128, 8 * 5632]) for l in range(2)]
    s_wdn = [dscr("s_wdn%d" % l, [128, 22 * 1024]) for l in range(2)]

    es = ExitStack()
    with es:
        def sb(name, shape, dt):
            return es.enter_context(nc.sbuf_tensor(name, list(shape), dt))

        P = Prog(nc, es)
        global LAST_PROG
        LAST_PROG = P
        x_sb = sb("x_sb", [128, NT, D], F32)
        hT = sb("hT", [128, 8, T], BF16)
        AR = sb("AR", [128, 23552], BF16)
        FA = sb("FA", [128, 2560], F32)
        WS = [sb("ws%d" % i, [128, 4096], BF16) for i in range(4)]
        xn = [sb("xn%d" % i, [128, D], BF16) for i in range(2)]
        junk = sb("junk", [128, D], BF16)
        rows = sb("rows_sb", [128, 2176], F32)
        cols = sb("cols_sb", [128, NCOL], F32)
        cst = sb("cst_sb", [128, 4, 128], BF16)
        wmT = sb("wmT", [128, 8, 128], BF16)
        c32 = sb("c32_sb", [128, 3, 128], F32)
        wf = sb("wf", [128, 8, 8], BF16)
        bout_b = sb("bout_b", [1, D], BF16)
        ss = sb("ss", [128, 32], F32)
        rstd = sb("rstd", [128, 32], F32)
        small = sb("small", [128, 64], F32)
        PS = [es.enter_context(nc.psum_tensor("ps%d" % i, [128, 512], F32)) for i in range(8)]
        PSB = [Buf() for _ in range(8)]

        wsT_f = FA[:, 0:1024].rearrange("p (a b) -> p a b", a=8)
        cstf = FA[:, 1024:1536].rearrange("p (a b) -> p a b", a=4)
        ident = cst[:, 0, :]
        maskadd = cst[:, 1, :]
        ones_b = cst[:, 3, :]

        def col(name, j=0, n=1, p0=0, p1=128):
            c0, _ = COLS[name]
            return cols[p0:p1, c0 + j:c0 + j + n]

        d_misc = P.dma_sem()
        B_misc = Buf()
        P.dma('pool', d_misc, bout_b[:], bout_d, writes=[B_misc])
        P.dma('pool', d_misc, wf[:], w_in_e.rearrange("(c p) n -> p c n", p=128)[:, :, 1536:1544], writes=[B_misc])
        for dst, src in [(c32[:], c32_d), (cols[:], cols_d), (rows[:], rows_d), (cstf, cst_d), (wsT_f, wsT_d),
                         ]:
            P.dma('sp', d_misc, dst, src, writes=[B_misc])
        B_cst = Buf()
        P.op('dve', lambda e: e.tensor_copy(out=cst[:], in_=cstf), reads=[B_misc], writes=[B_cst])
        for g in range(8):
            P.op('dve', lambda e, g=g: e.tensor_tensor(out=wmT[:, g, :], in0=wsT_f[:, g, :], in1=cstf[:, 2, :], op=ALU.mult),
                 reads=[B_misc], writes=[B_cst])

        WB = {}

        def cast(name, dst, src3, ncol, nchunk=8):
            k = P.dma_sem()
            b = Buf()
            d3 = dst.rearrange("p (c n) -> p c n", c=nchunk)
            for c0 in range(0, nchunk, 8):
                c1 = min(nchunk, c0 + 8)
                for n0 in range(0, ncol, 2048):
                    n1 = min(ncol, n0 + 2048)
                    P.dma('pool', k, d3[:, c0:c1, n0:n1], src3[:, c0:c1, n0:n1], writes=[b])
            WB[name] = b

        def kview(w):
            return w.rearrange("(c p) n -> p c n", p=128)

        def do_casts(which):
            for name in which:
                if name == 'w_in_e':
                    cast(name, s_w_in_e, kview(w_in_e), 2568)
                elif name == 'w_out_e':
                    cast(name, s_w_out_e, kview(w_out_e), 1024)
                elif name == 'conv_w_in':
                    cast(name, s_conv_w_in, kview(conv_w_in), 2048)
                elif name == 'conv_w_out':
                    cast(name, s_conv_w_out, kview(conv_w_out), 1024)
                elif name[:2] == 'wq':
                    l = int(name[2]); cast(name, s_wq[l], kview(xa_wq[l]), 1024)
                elif name[:3] == 'wkv':
                    l = int(name[3]); cast(name, s_wkv[l], kview(xa_wkv[l]), 2048)
                elif name[:2] == 'wo':
                    l = int(name[2]); cast(name, s_wo[l], kview(xa_wo[l]), 1024)
                elif name[:3] == 'wgu':
                    l = int(name[3]); cast(name, s_wgu[l], kview(ffn_w_gu[l]), 5632)
                elif name[:3] == 'wdn':
                    l = int(name[3]); cast(name, s_wdn[l], kview(ffn_w_down[l]), 1024, nchunk=22)

        def piece_src(spec):
            kind = spec[0]
            if kind == 'in_e':
                return 'w_in_e', s_w_in_e.rearrange("p (c n) -> p c n", c=8)[:, :, spec[1]:spec[1] + 512], (8, 512)
            if kind == 'out_e':
                return 'w_out_e', s_w_out_e.rearrange("p (c n) -> p c n", c=8)[:, spec[1]:spec[1] + 4, :], (4, 1024)
            if kind == 'cin':
                return 'conv_w_in', s_conv_w_in.rearrange("p (c n) -> p c n", c=8)[:, :, spec[1]:spec[1] + 512], (8, 512)
            if kind == 'cout':
                return 'conv_w_out', s_conv_w_out.rearrange("p (c n) -> p c n", c=8)[:, :, spec[1]:spec[1] + 512], (8, 512)
            if kind == 'wq':
                return 'wq%d' % spec[1], s_wq[spec[1]].rearrange("p (c n) -> p c n", c=8)[:, :, spec[2]:spec[2] + 512], (8, 512)
            if kind == 'wkv':
                return 'wkv%d' % spec[1], s_wkv[spec[1]].rearrange("p (c n) -> p c n", c=8)[:, :, spec[2]:spec[2] + 512], (8, 512)
            if kind == 'wo':
                return 'wo%d' % spec[1], s_wo[spec[1]].rearrange("p (c n) -> p c n", c=8)[:, :, spec[2]:spec[2] + 512], (8, 512)
            if kind == 'wgu':
                n = spec[3]
                return 'wgu%d' % spec[1], s_wgu[spec[1]].rearrange("p (c n) -> p c n", c=8)[:, :, spec[2]:spec[2] + n], (8, n)
            if kind == 'wdn':
                n = spec[3]
                return 'wdn%d' % spec[1], s_wdn[spec[1]].rearrange("p (c n) -> p c n", c=22)[:, spec[2]:spec[2] + n, :], (n, 1024)
            if kind == 'diag':
                return 'diag%d' % spec[1], s_diag[spec[1]].rearrange("p (a b) -> p a b", a=31), (31, 128)
            raise ValueError(spec)

        def seq_pieces():
            pl = []
            for st in stages:
                if st == 'mix0':
                    pl += [('in_e', 1544), ('in_e', 2056), ('out_e', 4), ('in_e', 1024), ('in_e', 0), ('in_e', 512), ('out_e', 0)]
                elif st in ('xa0', 'xa1'):
                    l = int(st[2])
                    pl += [('wq', l, 0), ('wq', l, 512), ('wo', l, 0), ('wo', l, 512)]
                elif st in ('ffn0', 'ffn1'):
                    l = int(st[3])
                    for gi in list(range(6)) * (2 if (st == stages[-1] and SPLIT_LAST) else 1):
                        nf = 4 if gi < 5 else 2
                        pl += [('wgu', l, gi * 512, nf * 128), ('wgu', l, FFN_H + gi * 512, nf * 128), ('wdn', l, gi * 4, nf)]
                elif st == 'mix1':
                    pl += [('cin', 0), ('cin', 1024), ('cin', 512), ('cin', 1536)] + [('diag', cc) for cc in range(8)] + [('cout', 0), ('cout', 512)]
            return pl

        kv_pieces = []
        for st in stages:
            if st[:2] == 'xa':
                l = int(st[2])
                kv_pieces += [('wkv', l, 0), ('wkv', l, 512), ('wkv', l, 1024), ('wkv', l, 1536)]
        all_pieces = list(kv_pieces)
        for s in range(NSEQ):
            all_pieces += seq_pieces()
            if s + 1 < NSEQ:
                all_pieces += kv_pieces

        class WStream:
            def __init__(self):
                self.next_load = 0
                self.next_acq = 0
                self.slot_of = {}
                self.sbuf = [Buf() for _ in WS]
                self.ssem = [P.dma_sem() for _ in WS]

            def _load(self, slot):
                if self.next_load >= len(all_pieces):
                    return
                idx = self.next_load
                self.next_load += 1
                name, src, (a, b) = piece_src(all_pieces[idx])
                dst = WS[slot][:, 0:a * b].rearrange("p (a b) -> p a b", a=a)
                P.dma('sp', self.ssem[slot], dst, src, reads=[WB[name]], writes=[self.sbuf[slot]])
                self.slot_of[idx] = slot

            def start(self):
                for s in range(len(WS)):
                    self._load(s)

            def acquire(self, spec):
                idx = self.next_acq
                assert all_pieces[idx] == spec, (all_pieces[idx], spec)
                assert idx in self.slot_of, "weight slot deadlock at piece %d %s" % (idx, spec)
                self.next_acq += 1
                slot = self.slot_of[idx]
                _, _, (a, b) = piece_src(spec)
                return slot, WS[slot][:, 0:a * b].rearrange("p (a b) -> p a b", a=a), self.sbuf[slot]

            def release(self, slot):
                self._load(slot)

        psrot = Rot(range(8))
        d_x = [P.dma_sem() for _ in range(NT)]
        XB = [Buf() for _ in range(NT)]
        HB = [Buf() for _ in range(4)]
        B_xn = [Buf(), Buf()]
        B_small = Buf()
        xn_rot = Rot([0, 1])

        def norm_stats(tiles_ap_fn, n, bufs, ss_off):
            for i in range(n):
                P.op('act', lambda e, i=i: e.activation(out=junk[:], in_=tiles_ap_fn(i), func=AF.Square,
                                                        accum_out=ss[:, ss_off + i:ss_off + i + 1]),
                     reads=[bufs[i]], writes=[B_small])
            P.op('act', lambda e: e.activation(out=ss[:, ss_off:ss_off + n], in_=ss[:, ss_off:ss_off + n], func=AF.Sqrt,
                                               scale=1.0 / D, bias=col_eps()),
                 reads=[B_small, B_cst], writes=[B_small])
            P.op('dve', lambda e: e.reciprocal(out=rstd[:, ss_off:ss_off + n], in_=ss[:, ss_off:ss_off + n]),
                 reads=[B_small], writes=[B_small])

        eps_t = sb("eps_t", [128, 1], F32)
        P.op('dve', lambda e: e.memset(eps_t[:], EPS), writes=[B_cst])

        def col_eps():
            return eps_t[:, 0:1]

        def norm_to_hT(gname):
            for tb in range(4):
                norm_group(gname, tb)

        def norm_group(gname, tb):
            norm_group_stats(tb)
            norm_group_apply(gname, tb)

        def norm_group_stats(tb):
            norm_stats(lambda i, tb=tb: x_sb[:, tb * 4 + i, :], 4, XB[tb * 4:tb * 4 + 4], tb * 4)

        def norm_hook(gname, tb):
            norm_group_stats(tb)
            if tb >= 1:
                norm_group_apply(gname, tb - 1)
            if tb == 3:
                norm_group_apply(gname, 3)

        def norm_group_apply(gname, tb):
            if True:
                banks = [psrot.next() for _ in range(4)]
                for i in range(4):
                    ti = tb * 4 + i
                    xi = xn_rot.next()
                    P.op('pool', lambda e, ti=ti, xi=xi: e.tensor_scalar(out=xn[xi][:], in0=x_sb[:, ti, :],
                                                                        scalar1=rstd[:, ti:ti + 1], scalar2=0.0,
                                                                        op0=ALU.mult, op1=ALU.add),
                         reads=[XB[ti], B_small], writes=[B_xn[xi]])
                    for c in range(8):
                        bk = banks[c // 2]
                        o = PS[bk][:].bitcast(BF16)[:, (c % 2) * 512 + i * 128:(c % 2) * 512 + (i + 1) * 128]
                        P.op('pe', lambda e, o=o, xi=xi, c=c: e.transpose(out=o, in_=xn[xi][:, c * 128:(c + 1) * 128], identity=ident),
                             reads=[B_xn[xi], B_cst], writes=[PSB[bk]])
                for c in range(8):
                    bk = banks[c // 2]
                    src = PS[bk][:].bitcast(BF16)[:, (c % 2) * 512:(c % 2) * 512 + 512]
                    eng = 'act' if c % 2 == 0 else 'dve'
                    if eng == 'act':
                        P.op('act', lambda e, src=src, c=c, tb=tb: e.activation(out=hT[:, c, tb * 512:(tb + 1) * 512], in_=src,
                                                                               func=AF.Identity, scale=col(gname, c)),
                             reads=[PSB[bk], B_misc], writes=[HB[tb]])
                    else:
                        P.op('dve', lambda e, src=src, c=c, tb=tb: e.tensor_scalar(out=hT[:, c, tb * 512:(tb + 1) * 512], in0=src,
                                                                                  scalar1=col(gname, c), scalar2=None, op0=ALU.mult),
                             reads=[PSB[bk], B_misc], writes=[HB[tb]])

        def resid_add(ti, half, bk):
            P.op('dve', lambda e: e.tensor_tensor(out=x_sb[:, ti, half * 512:(half + 1) * 512], in0=PS[bk][:],
                                                  in1=x_sb[:, ti, half * 512:(half + 1) * 512], op=ALU.add),
                 reads=[PSB[bk]], writes=[XB[ti]])

        W = WStream()

        def stage_ffn(l, hook, pre, s=0, tbs=(0, 1, 2, 3)):
            if not pre:
                norm_to_hT('ffn%d' % l)
            if stages[-1] == 'ffn%d' % l and s + 1 < NSEQ:
                kv_mem_load(s + 1)
            fb = P.fence()
            B_hid = Buf(fb)
            B_sg = [Buf(fb), Buf(fb)]
            hid = AR[:, 0:8192].rearrange("p (a b) -> p a b", a=4)
            sg = [AR[:, 8192 + i * 512:8192 + (i + 1) * 512] for i in range(2)]
            sgr = Rot([0, 1])
            for gi in range(6):
                nf = 4 if gi < 5 else 2
                sl_g, wg, bg = W.acquire(('wgu', l, gi * 512, nf * 128))
                sl_u, wu, bu = W.acquire(('wgu', l, FFN_H + gi * 512, nf * 128))
                sl_d, wd, bd = W.acquire(('wdn', l, gi * 4, nf))
                for jj in range(nf):
                    for tb in tbs:
                        ba, bb = psrot.next(), psrot.next()
                        for c in range(8):
                            P.op('pe', lambda e, ba=ba, c=c, jj=jj, tb=tb, wg=wg: e.matmul(
                                PS[ba][:], lhsT=wg[:, c, jj * 128:(jj + 1) * 128], rhs=hT[:, c, tb * 512:(tb + 1) * 512],
                                start=(c == 0), stop=(c == 7)), reads=[bg, HB[tb]], writes=[PSB[ba]])
                        for c in range(8):
                            P.op('pe', lambda e, bb=bb, c=c, jj=jj, tb=tb, wu=wu: e.matmul(
                                PS[bb][:], lhsT=wu[:, c, jj * 128:(jj + 1) * 128], rhs=hT[:, c, tb * 512:(tb + 1) * 512],
                                start=(c == 0), stop=(c == 7)), reads=[bu, HB[tb]], writes=[PSB[bb]])
                        si = sgr.next()
                        P.op('act', lambda e, ba=ba, si=si: e.activation(out=sg[si], in_=PS[ba][:], func=AF.Silu),
                             reads=[PSB[ba]], writes=[B_sg[si]])
                        P.op('dve', lambda e, bb=bb, si=si, jj=jj, tb=tb: e.tensor_tensor(
                            out=hid[:, jj, tb * 512:(tb + 1) * 512], in0=PS[bb][:], in1=sg[si], op=ALU.mult),
                            reads=[PSB[bb], B_sg[si]], writes=[B_hid])
                W.release(sl_g)
                W.release(sl_u)
                for ti in range(tbs[0] * 4, tbs[-1] * 4 + 4):
                    for half in range(2):
                        bk = psrot.next()
                        for jj in range(nf):
                            P.op('pe', lambda e, bk=bk, jj=jj, ti=ti, half=half, wd=wd: e.matmul(
                                PS[bk][:], lhsT=hid[:, jj, ti * 128:(ti + 1) * 128], rhs=wd[:, jj, half * 512:(half + 1) * 512],
                                start=(jj == 0), stop=(jj == nf - 1)), reads=[bd, B_hid], writes=[PSB[bk]])
                        resid_add(ti, half, bk)
                    if gi == 5 and ti % 4 == 3:
                        hook(ti // 4)
                W.release(sl_d)

        d_mem = P.dma_sem()
        B_kvs = [Buf(), Buf()]
        d_kvs = [P.dma_sem(), P.dma_sem()]
        d_kvl = P.dma_sem()
        xa_layers = [int(st[2]) for st in stages if st[:2] == 'xa']

        memH = {}

        def kv_mem_load(s):
            if not xa_layers or s in memH:
                return
            memH[s] = Buf(P.fence())
            memt = FA[:, 0:2048].rearrange("p (a b) -> p a b", a=2)
            P.dma('sp', d_mem, memt, mem_d[s].rearrange("(a p) d -> p a d", p=128), writes=[memH[s]])

        def kv_prep(s):
            if not xa_layers:
                return
            kv_mem_load(s)
            fb = P.fence(engines_only=(s > 0))
            B_mem = memH[s]; B_mT = Buf(fb); B_kT = Buf(fb); B_V = Buf(fb)
            memt = FA[:, 0:2048].rearrange("p (a b) -> p a b", a=2)
            mT = AR[:, 0:2048].rearrange("p (a b) -> p a b", a=8)
            kvt = AR[:, 2048:6144]
            kT = AR[:, 2048:4096].rearrange("p (a b) -> p a b", a=8)
            Vt = AR[:, 4096:6144].rearrange("p (a b) -> p a b", a=2)
            norm_stats(lambda i: memt[:, i, :], 2, [B_mem, B_mem], 16)
            for l in xa_layers:
                bk = psrot.next()
                bk2 = psrot.next()
                for i in range(2):
                    xi = xn_rot.next()
                    P.op('pool', lambda e, i=i, xi=xi: e.tensor_scalar(out=xn[xi][:], in0=memt[:, i, :], scalar1=rstd[:, 16 + i:17 + i],
                                                                      scalar2=0.0, op0=ALU.mult, op1=ALU.add),
                         reads=[B_mem, B_small], writes=[B_xn[xi]])
                    for c in range(8):
                        b_ = bk if c < 4 else bk2
                        o = PS[b_][:].bitcast(BF16)[:, (c % 4) * 256 + i * 128:(c % 4) * 256 + (i + 1) * 128]
                        P.op('pe', lambda e, o=o, xi=xi, c=c: e.transpose(out=o, in_=xn[xi][:, c * 128:(c + 1) * 128], identity=ident),
                             reads=[B_xn[xi], B_cst], writes=[PSB[b_]])
                for c in range(8):
                    b_ = bk if c < 4 else bk2
                    src = PS[b_][:].bitcast(BF16)[:, (c % 4) * 256:(c % 4) * 256 + 256]
                    P.op('act', lambda e, src=src, c=c, l=l: e.activation(out=mT[:, c, :], in_=src, func=AF.Identity, scale=col('mem%d' % l, c)),
                         reads=[PSB[b_], B_misc], writes=[B_mT])
                for half in range(2):
                    sl, wk, bw = W.acquire(('wkv', l, half * 512))
                    for jj in range(4):
                        j = half * 4 + jj
                        b_ = psrot.next()
                        for c in range(8):
                            P.op('pe', lambda e, b_=b_, c=c, jj=jj, wk=wk: e.matmul(PS[b_][:, 0:256], lhsT=wk[:, c, jj * 128:(jj + 1) * 128],
                                                                                   rhs=mT[:, c, :], start=(c == 0), stop=(c == 7)),
                                 reads=[bw, B_mT], writes=[PSB[b_]])
                        P.op('act', lambda e, b_=b_, j=j: e.copy(out=kT[:, j, :], in_=PS[b_][:, 0:256]), reads=[PSB[b_]], writes=[B_kT])
                    W.release(sl)
                for half in range(2):
                    sl, wv, bw = W.acquire(('wkv', l, 1024 + half * 512))
                    for mt in range(2):
                        b_ = psrot.next()
                        for c in range(8):
                            P.op('pe', lambda e, b_=b_, c=c, mt=mt, wv=wv: e.matmul(PS[b_][:], lhsT=mT[:, c, mt * 128:(mt + 1) * 128],
                                                                                   rhs=wv[:, c, :], start=(c == 0), stop=(c == 7)),
                                 reads=[bw, B_mT], writes=[PSB[b_]])
                        P.op('act', lambda e, b_=b_, mt=mt, half=half: e.copy(out=Vt[:, mt, half * 512:(half + 1) * 512], in_=PS[b_][:]),
                             reads=[PSB[b_]], writes=[B_V])
                    W.release(sl)
                P.dma('sp', d_kvs[l], kv_scr[l], kvt, reads=[B_kT, B_V], writes=[B_kvs[l]])

        def stage_xa(l, s, hook, pre):
            fb = P.fence()
            B_kT = Buf(fb); B_V = B_kT
            qTb = [AR[:, i * 4096:(i + 1) * 4096].rearrange("p (a b) -> p a b", a=8) for i in range(2)]
            oTb = [AR[:, 8192 + i * 4096:8192 + (i + 1) * 4096].rearrange("p (a b) -> p a b", a=8) for i in range(2)]
            kT = AR[:, 16384:18432].rearrange("p (a b) -> p a b", a=8)
            Vt = AR[:, 18432:20480].rearrange("p (a b) -> p a b", a=2)
            PT = [AR[:, 20480 + i * 512:20480 + (i + 1) * 512] for i in range(6)]
            B_qT = [Buf(fb), Buf(fb)]; B_oT = [Buf(fb), Buf(fb)]
            P.dma('sp', d_kvl, AR[:, 16384:20480], kv_scr[l], reads=[B_kvs[l]], writes=[B_kT])
            if not pre:
                norm_to_hT('xa%d' % l)
            slq = [W.acquire(('wq', l, 0)), W.acquire(('wq', l, 512))]
            slo = [W.acquire(('wo', l, 0)), W.acquire(('wo', l, 512))]
            rden = FA[:, 0:512]
            fbx = P.fence()
            B_rden = Buf(fbx)
            B_PT = [Buf(fbx) for _ in range(6)]

            def qproj(tb):
                qT = qTb[tb % 2]
                for j in range(8):
                    _, wq, bw = slq[j // 4]
                    b_ = psrot.next()
                    for c in range(8):
                        P.op('pe', lambda e, b_=b_, c=c, j=j, wq=wq, tb=tb: e.matmul(
                            PS[b_][:], lhsT=wq[:, c, (j % 4) * 128:(j % 4 + 1) * 128], rhs=hT[:, c, tb * 512:(tb + 1) * 512],
                            start=(c == 0), stop=(c == 7)), reads=[bw, HB[tb]], writes=[PSB[b_]])
                    P.op('act', lambda e, b_=b_, j=j, qT=qT: e.activation(out=qT[:, j, :], in_=PS[b_][:], func=AF.Copy, scale=1.0 / 16.0),
                         reads=[PSB[b_]], writes=[B_qT[tb % 2]])

            def s1(tb, hd):
                qT = qTb[tb % 2]
                for mc in range(2):
                    b_ = psrot.next()
                    pi = (hd % 3) * 2 + mc
                    for dd in range(2):
                        j = hd * 2 + dd
                        P.op('pe', lambda e, b_=b_, j=j, mc=mc, dd=dd, qT=qT: e.matmul(
                            PS[b_][:], lhsT=kT[:, j, mc * 128:(mc + 1) * 128], rhs=qT[:, j, :],
                            start=(dd == 0), stop=(dd == 1)), reads=[B_kT, B_qT[tb % 2]], writes=[PSB[b_]])
                    P.op('act', lambda e, b_=b_, pi=pi: e.activation(out=PT[pi], in_=PS[b_][:], func=AF.Exp),
                         reads=[PSB[b_]], writes=[B_PT[pi]])

            def s2(tb, hd):
                oT = oTb[tb % 2]
                pts = [(hd % 3) * 2, (hd % 3) * 2 + 1]
                bden = psrot.next()
                for mc in range(2):
                    P.op('pe', lambda e, bden=bden, mc=mc, pi=pts[mc]: e.matmul(PS[bden][:], lhsT=ones_b, rhs=PT[pi],
                                                                               start=(mc == 0), stop=(mc == 1)),
                         reads=[B_PT[pts[mc]], B_cst], writes=[PSB[bden]])
                P.op('act', lambda e, bden=bden: e.activation(out=rden, in_=PS[bden][:], func=AF.Ln), reads=[PSB[bden]], writes=[B_rden])
                P.op('act', lambda e: e.activation(out=rden, in_=rden, func=AF.Exp, scale=-1.0), reads=[B_rden], writes=[B_rden])
                for dd in range(2):
                    j = hd * 2 + dd
                    b_ = psrot.next()
                    for mc in range(2):
                        P.op('pe', lambda e, b_=b_, j=j, mc=mc, pi=pts[mc]: e.matmul(
                            PS[b_][:], lhsT=Vt[:, mc, j * 128:(j + 1) * 128], rhs=PT[pi], start=(mc == 0), stop=(mc == 1)),
                            reads=[B_V, B_PT[pts[mc]]], writes=[PSB[b_]])
                    P.op('dve', lambda e, b_=b_, j=j, oT=oT: e.tensor_tensor(out=oT[:, j, :], in0=PS[b_][:], in1=rden, op=ALU.mult),
                         reads=[PSB[b_], B_rden], writes=[B_oT[tb % 2]])

            def wo_proj(tb):
                oT = oTb[tb % 2]
                for i in range(4):
                    ti = tb * 4 + i
                    for half in range(2):
                        _, wo, bw = slo[half]
                        b_ = psrot.next()
                        for j in range(8):
                            P.op('pe', lambda e, b_=b_, j=j, i=i, wo=wo, oT=oT: e.matmul(PS[b_][:], lhsT=oT[:, j, i * 128:(i + 1) * 128],
                                                                                        rhs=wo[:, j, :], start=(j == 0), stop=(j == 7)),
                                 reads=[bw, B_oT[tb % 2]], writes=[PSB[b_]])
                        resid_add(ti, half, b_)

            qproj(0)
            for tb in range(4):
                s1(tb, 0)
                for hd in range(4):
                    if hd + 1 < 4:
                        s1(tb, hd + 1)
                    s2(tb, hd)
                    if hd == 1 and tb + 1 < 4:
                        qproj(tb + 1)
                        if tb + 1 == 3:
                            for sl, _, _ in slq:
                                W.release(sl)
                if tb > 0:
                    wo_proj(tb - 1)
                    hook(tb - 1)
            wo_proj(3)
            hook(3)
            for sl, _, _ in slo:
                W.release(sl)


        B_cumd = Buf()
        d_cum = P.dma_sem()
        d_kE = P.dma_sem(); d_kO = P.dma_sem()
        d_qE = [P.dma_sem(), P.dma_sem()]; d_qO = [P.dma_sem(), P.dma_sem()]

        def stage_mix0(s, hook, pre):
            fb = P.fence()
            vgn = AR[:, 0:8192].rearrange("p (a b) -> p a b", a=16)
            u_sb = AR[:, 8192:16384].rearrange("p (a b) -> p a b", a=16)
            kE = AR[:, 16384:18432]; kO = AR[:, 18432:20480]
            qE = [AR[:, 20480 + i * 512:20480 + (i + 1) * 512] for i in range(2)]
            qO = [AR[:, 21504 + i * 512:21504 + (i + 1) * 512] for i in range(2)]
            PTa = [AR[:, 22528 + i * 512:22528 + (i + 1) * 512] for i in range(2)]
            aT = [PTa[i].rearrange("p (a b) -> p a b", a=4) for i in range(2)]
            vgf = [FA[:, 0:512], FA[:, 512:1024]]
            tmpv = FA[:, 1024:1536]
            rden = FA[:, 1536:2048]
            small2 = FA[:, 2048:2560]
            sp6 = AR[:, 16384:17152].rearrange("p (a b) -> p a b", a=6)
            st6 = AR[:, 17152:17920]
            B_vgn = Buf(fb); B_u = Buf(fb); B_vgf = [Buf(fb), Buf(fb)]; B_tmpv = Buf(fb); B_aT = [Buf(fb), Buf(fb)]
            B_f = Buf(fb)
            slu, wzu, bzu = W.acquire(('in_e', 1544))
            slv, wzv, bzv = W.acquire(('in_e', 2056))
            def zproj(ti):
                ba, bb = psrot.next(), psrot.next()
                for c in range(8):
                    P.op('pe', lambda e, ba=ba, c=c, ti=ti: e.matmul(PS[ba][:], lhsT=hT[:, c, ti * 128:(ti + 1) * 128], rhs=wzu[:, c, :],
                                                                   start=(c == 0), stop=(c == 7)), reads=[bzu, HB[ti // 4]], writes=[PSB[ba]])
                for c in range(8):
                    P.op('pe', lambda e, bb=bb, c=c, ti=ti: e.matmul(PS[bb][:], lhsT=hT[:, c, ti * 128:(ti + 1) * 128], rhs=wzv[:, c, :],
                                                                   start=(c == 0), stop=(c == 7)), reads=[bzv, HB[ti // 4]], writes=[PSB[bb]])
                k = ti % 2
                so = 32 + k * 16
                P.op('act', lambda e, ba=ba, ti=ti: e.activation(out=u_sb[:, ti, :], in_=PS[ba][:], func=AF.Gelu_apprx_tanh),
                     reads=[PSB[ba]], writes=[B_u])
                P.op('act', lambda e, bb=bb, k=k: e.activation(out=vgf[k], in_=PS[bb][:], func=AF.Gelu_apprx_tanh),
                     reads=[PSB[bb]], writes=[B_vgf[k]])
                P.op('dve', lambda e, k=k, so=so: e.bn_stats(out=small[:, so:so + 6], in_=vgf[k]), reads=[B_vgf[k]], writes=[B_small])
                P.op('dve', lambda e, so=so: e.bn_aggr(out=small[:, so + 6:so + 8], in_=small[:, so:so + 6]), reads=[B_small], writes=[B_small])
                P.op('act', lambda e, so=so: e.activation(out=small[:, so + 8:so + 9], in_=small[:, so + 7:so + 8], func=AF.Sqrt, bias=col_eps()),
                     reads=[B_small, B_cst], writes=[B_small])
                P.op('dve', lambda e, so=so: e.reciprocal(out=small[:, so + 9:so + 10], in_=small[:, so + 8:so + 9]), reads=[B_small], writes=[B_small])
                P.op('dve', lambda e, k=k, so=so: e.scalar_tensor_tensor(out=tmpv, in0=vgf[k], scalar=small[:, so + 6:so + 7], in1=rows[:, 1024:1536],
                                                                        op0=ALU.subtract, op1=ALU.mult),
                     reads=[B_vgf[k], B_small, B_misc], writes=[B_tmpv])
                P.op('dve', lambda e, ti=ti, so=so: e.scalar_tensor_tensor(out=vgn[:, ti, :], in0=tmpv, scalar=small[:, so + 9:so + 10], in1=rows[:, 1536:2048],
                                                                          op0=ALU.mult, op1=ALU.add),
                     reads=[B_tmpv, B_small, B_misc], writes=[B_vgn])

            if pre:
                for ti in range(NT):
                    zproj(ti)
            else:
                norm_group_stats(0)
                for g in range(4):
                    if g + 1 < 4:
                        norm_group_stats(g + 1)
                    norm_group_apply('mix_e', g)
                    if g >= 1:
                        for ti in range((g - 1) * 4, g * 4):
                            zproj(ti)
                for ti in range(12, 16):
                    zproj(ti)
            W.release(slu)
            W.release(slv)
            bf_ = psrot.next()
            for ti in range(NT):
                for c in range(8):
                    P.op('pe', lambda e, c=c, ti=ti: e.matmul(PS[bf_][:, ti * 8:(ti + 1) * 8], lhsT=hT[:, c, ti * 128:(ti + 1) * 128], rhs=wf[:, c, :],
                                                            start=(c == 0), stop=(c == 7)), reads=[B_misc, HB[ti // 4]], writes=[PSB[bf_]])
            sp32 = small2[:, 0:128]
            P.op('dve', lambda e: e.tensor_tensor(out=sp32, in0=PS[bf_][:, 0:128], in1=rows[:, 2048:2176], op=ALU.add),
                 reads=[PSB[bf_], B_misc], writes=[B_f])
            P.op('act', lambda e: e.activation(out=sp32, in_=sp32, func=AF.Exp, scale=-1.0), reads=[B_f], writes=[B_f])
            P.op('act', lambda e: e.activation(out=sp32, in_=sp32, func=AF.Ln, bias=1.0), reads=[B_f], writes=[B_f])
            slw, woG, bwoG = W.acquire(('out_e', 4))
            for half in range(2):
                for g in range(8):
                    b_ = psrot.next()
                    P.op('pe', lambda e, b_=b_, g=g, half=half: e.matmul(PS[b_][:], lhsT=wmT[:, g, :],
                                                                       rhs=vgn[:, half * 8:(half + 1) * 8, g * 64:(g + 1) * 64],
                                                                       start=True, stop=True), reads=[B_vgn, B_cst], writes=[PSB[b_]])
                    P.op('dve', lambda e, b_=b_, g=g, half=half: e.scalar_tensor_tensor(
                        out=u_sb[:, half * 8:(half + 1) * 8, g * 64:(g + 1) * 64],
                        in0=PS[b_][:].rearrange("p (a b) -> p a b", a=8), scalar=col('gbs', g),
                        in1=u_sb[:, half * 8:(half + 1) * 8, g * 64:(g + 1) * 64], op0=ALU.add, op1=ALU.mult),
                        reads=[PSB[b_], B_misc], writes=[B_u])
            bt_ = psrot.next()
            P.op('pe', lambda e: e.matmul(PS[bt_][:, 0:128], lhsT=sp32, rhs=c32[:, 1, :], start=True, stop=True),
                 reads=[B_f, B_misc], writes=[PSB[bt_]])
            totrep = small2[:, 128:256]
            P.op('dve', lambda e: e.tensor_copy(out=totrep, in_=PS[bt_][:, 0:128]), reads=[PSB[bt_]], writes=[B_f])
            bc_ = psrot.next()
            P.op('pe', lambda e: e.matmul(PS[bc_][:, 0:128], lhsT=c32[:, 0, :], rhs=sp32, start=True, stop=False),
                 reads=[B_f, B_misc], writes=[PSB[bc_]])
            P.op('pe', lambda e: e.matmul(PS[bc_][:, 0:128], lhsT=totrep, rhs=c32[:, 2, :], start=False, stop=True),
                 reads=[B_f, B_misc], writes=[PSB[bc_]])
            ncum = small2[:, 256:384]
            r1 = small2[:, 384:512]
            P.op('dve', lambda e: e.tensor_copy(out=ncum, in_=PS[bc_][:, 0:128]), reads=[PSB[bc_]], writes=[B_f])
            P.op('pool', lambda e: e.tensor_copy(out=sp6[:, 3, :], in_=ncum), reads=[B_f], writes=[B_f])
            P.op('pool', lambda e: e.tensor_tensor(out=r1, in0=ncum, in1=sp6[:, 3, :], op=ALU.subtract), reads=[B_f], writes=[B_f])
            P.op('pool', lambda e: e.tensor_copy(out=sp6[:, 4, :], in_=r1), reads=[B_f], writes=[B_f])
            P.op('pool', lambda e: e.tensor_tensor(out=r1, in0=r1, in1=sp6[:, 4, :], op=ALU.subtract), reads=[B_f], writes=[B_f])
            P.op('pool', lambda e: e.tensor_copy(out=sp6[:, 5, :], in_=r1), reads=[B_f], writes=[B_f])
            P.op('pool', lambda e: e.tensor_scalar(out=sp6[:, 0:3, :], in0=sp6[:, 3:6, :], scalar1=-1.0, scalar2=0.0, op0=ALU.mult, op1=ALU.add),
                 reads=[B_f], writes=[B_f])
            for ti in range(NT):
                k = ti % 2
                b_ = psrot.next()
                for cg in range(4):
                    o = PS[b_][:].bitcast(BF16)[:, cg * 128:(cg + 1) * 128]
                    P.op('pe', lambda e, o=o, ti=ti, cg=cg: e.transpose(out=o, in_=u_sb[:, ti, cg * 128:(cg + 1) * 128], identity=ident),
                         reads=[B_u, B_cst], writes=[PSB[b_]])
                P.op('dve', lambda e, b_=b_, k=k: e.tensor_copy(out=PTa[k], in_=PS[b_][:].bitcast(BF16)[:, 0:512]), reads=[PSB[b_]], writes=[B_aT[k]])
                for half in range(2):
                    b2 = psrot.next()
                    for cg in range(4):
                        P.op('pe', lambda e, b2=b2, cg=cg, k=k, half=half: e.matmul(PS[b2][:], lhsT=aT[k][:, cg, :],
                                                                                  rhs=woG[:, cg, half * 512:(half + 1) * 512],
                                                                                  start=(cg == 0), stop=(cg == 3)),
                             reads=[B_aT[k], bwoG], writes=[PSB[b2]])
                    resid_add(ti, half, b2)
            W.release(slw)
            bx_ = psrot.next()
            for k6 in range(6):
                o = PS[bx_][:].bitcast(BF16)[:, k6 * 128:(k6 + 1) * 128]
                P.op('pe', lambda e, o=o, k6=k6: e.transpose(out=o, in_=sp6[:, k6, :], identity=ident), reads=[B_f, B_cst], writes=[PSB[bx_]])
            P.op('act', lambda e: e.copy(out=st6, in_=PS[bx_][:].bitcast(BF16)[:, 0:768]), reads=[PSB[bx_]], writes=[B_f])
            P.dma('sp', d_cum, cumd, st6, reads=[B_f], writes=[B_cumd])
            cum4 = cumd.rearrange("(j h) (k t) -> h k j t", h=8, k=6)
            fb2 = P.fence()
            vt = AR[:, 0:8192].rearrange("p (a b) -> p a b", a=16)
            foxT = AR[:, 8192:16384].rearrange("p (a b) -> p a b", a=4)
            B_v = Buf(fb2); B_fox = Buf(fb2); B_PT = [Buf(fb2), Buf(fb2)]; B_rden = Buf(fb2)
            B_kE = Buf(fb2); B_kO = Buf(fb2); B_qE = [Buf(fb2), Buf(fb2)]; B_qO = [Buf(fb2), Buf(fb2)]
            slv2, wv, bwv = W.acquire(('in_e', 1024))
            for ti in range(NT):
                b_ = psrot.next()
                for c in range(8):
                    P.op('pe', lambda e, b_=b_, c=c, ti=ti: e.matmul(PS[b_][:], lhsT=hT[:, c, ti * 128:(ti + 1) * 128], rhs=wv[:, c, :],
                                                                   start=(c == 0), stop=(c == 7)), reads=[bwv, HB[ti // 4]], writes=[PSB[b_]])
                P.op('dve', lambda e, b_=b_, ti=ti: e.tensor_copy(out=vt[:, ti, :], in_=PS[b_][:]), reads=[PSB[b_]], writes=[B_v])
            W.release(slv2)
            for tl, bb_ in [(kE, B_kE), (qE[0], B_qE[0]), (qE[1], B_qE[1])]:
                P.op('pool', lambda e, tl=tl: e.memset(tl[64:70, :], 1.0), writes=[bb_])
            for tl, bb_ in [(kO, B_kO), (qO[0], B_qO[0]), (qO[1], B_qO[1])]:
                P.op('pool', lambda e, tl=tl: e.memset(tl[0:64, :], 0.0), writes=[bb_])
                P.op('pool', lambda e, tl=tl: e.memset(tl[0:6, :], 1.0), writes=[bb_])
            slq, wq, bwq = W.acquire(('in_e', 0))
            slk, wk, bwk = W.acquire(('in_e', 512))
            slf, woF, bwoF = W.acquire(('out_e', 0))
            strot = Rot([4, 5, 6, 7])
            PTl = [PTa[0], PTa[1]] + [FA[:, i * 256:(i + 1) * 256].bitcast(BF16) for i in range(4)]
            B_PTl = [Buf(fb2) for _ in range(6)]
            LA = 3
            unit_ctr = [0]
            for hp in range(4):
                hE, hO = 2 * hp, 2 * hp + 1
                for tb in range(4):
                    b_ = strot.next()
                    for c in range(8):
                        P.op('pe', lambda e, b_=b_, c=c, tb=tb, hp=hp: e.matmul(PS[b_][:], lhsT=wk[:, c, hp * 128:(hp + 1) * 128],
                                                                              rhs=hT[:, c, tb * 512:(tb + 1) * 512], start=(c == 0), stop=(c == 7)),
                             reads=[bwk, HB[tb]], writes=[PSB[b_]])
                    P.op('act', lambda e, b_=b_, tb=tb: e.copy(out=kE[0:64, tb * 512:(tb + 1) * 512], in_=PS[b_][0:64, :]),
                         reads=[PSB[b_]], writes=[B_kE])
                    P.op('act', lambda e, b_=b_, tb=tb: e.copy(out=kO[64:128, tb * 512:(tb + 1) * 512], in_=PS[b_][64:128, :]),
                         reads=[PSB[b_]], writes=[B_kO])
                P.dma('sp', d_kE, kE[67:70, :].rearrange("p (j t) -> p j t", j=16), cum4[hE, 3:6], reads=[B_cumd], writes=[B_kE])
                P.dma('sp', d_kO, kO[3:6, :].rearrange("p (j t) -> p j t", j=16), cum4[hO, 3:6], reads=[B_cumd], writes=[B_kO])

                def qproj(QB, hp=hp, hE=hE, hO=hO):
                    qi = QB % 2
                    b_ = strot.next()
                    for c in range(8):
                        P.op('pe', lambda e, b_=b_, c=c, QB=QB, hp=hp: e.matmul(PS[b_][:], lhsT=wq[:, c, hp * 128:(hp + 1) * 128],
                                                                              rhs=hT[:, c, QB * 512:(QB + 1) * 512], start=(c == 0), stop=(c == 7)),
                             reads=[bwq, HB[QB]], writes=[PSB[b_]])
                    P.op('act', lambda e, b_=b_, qi=qi: e.activation(out=qE[qi][0:64, :], in_=PS[b_][0:64, :], func=AF.Copy, scale=0.125),
                         reads=[PSB[b_]], writes=[B_qE[qi]])
                    P.op('act', lambda e, b_=b_, qi=qi: e.activation(out=qO[qi][64:128, :], in_=PS[b_][64:128, :], func=AF.Copy, scale=0.125),
                         reads=[PSB[b_]], writes=[B_qO[qi]])
                    P.dma('sp', d_qE[qi], qE[qi][64:67, :].rearrange("p (j t) -> p j t", j=4), cum4[hE, 0:3, QB * 4:(QB + 1) * 4],
                          reads=[B_cumd], writes=[B_qE[qi]])
                    P.dma('sp', d_qO[qi], qO[qi][0:3, :].rearrange("p (j t) -> p j t", j=4), cum4[hO, 0:3, QB * 4:(QB + 1) * 4],
                          reads=[B_cumd], writes=[B_qO[qi]])

                tasks = []
                for QB in range(4):
                    for par in range(2):
                        bo_, bd_ = (0, 1) if unit_ctr[0] % 2 == 0 else (2, 3)
                        unit_ctr[0] += 1
                        for kc in range(4 * QB + 4):
                            tasks.append((QB, par, kc, bo_, bd_))

                def operands(QB, par):
                    qi = QB % 2
                    if par == 0:
                        return kE[0:70, :], qE[qi][0:70, :], B_kE, B_qE[qi], 0, 64, hE * 64
                    return kO, qO[qi], B_kO, B_qO[qi], 64, 128, (hO - 1) * 64

                def st1(i):
                    QB, par, kc, bo_, bd_ = tasks[i]
                    if par == 0 and kc == 0 and QB + 1 < 4:
                        qproj(QB + 1)
                    kt, qt, bk_, bq_, r0, r1_, vlo = operands(QB, par)
                    r = kc - 4 * QB
                    n0 = max(r, 0) * 128
                    bs_ = strot.next()
                    P.op('pe', lambda e, bs_=bs_, kc=kc, n0=n0, r=r, kt=kt, qt=qt: e.matmul(
                        PS[bs_][:, n0:512], lhsT=kt[:, kc * 128:(kc + 1) * 128], rhs=qt[:, n0:512], start=True, stop=(r < 0)),
                        reads=[bk_, bq_], writes=[PSB[bs_]])
                    if r >= 0:
                        P.op('pe', lambda e, bs_=bs_, n0=n0: e.matmul(PS[bs_][:, n0:n0 + 128], lhsT=ident, rhs=maskadd, start=False, stop=True),
                             reads=[B_cst], writes=[PSB[bs_]])
                    pi = i % 6
                    P.op('act', lambda e, bs_=bs_, pi=pi, n0=n0: e.activation(out=PTl[pi][:, n0:512], in_=PS[bs_][:, n0:512], func=AF.Exp),
                         reads=[PSB[bs_]], writes=[B_PTl[pi]])

                def st2(i):
                    QB, par, kc, bo_, bd_ = tasks[i]
                    kt, qt, bk_, bq_, r0, r1_, vlo = operands(QB, par)
                    r = kc - 4 * QB
                    n0 = max(r, 0) * 128
                    nkc = 4 * QB + 4
                    pi = i % 6
                    P.op('pe', lambda e, kc=kc, pi=pi, n0=n0, vlo=vlo, nkc=nkc, bo_=bo_: e.matmul(
                        PS[bo_][:, n0:512], lhsT=vt[:, kc, vlo:vlo + 128], rhs=PTl[pi][:, n0:512], start=(kc == 0), stop=(kc == nkc - 1)),
                        reads=[B_v, B_PTl[pi]], writes=[PSB[bo_]])
                    P.op('pe', lambda e, kc=kc, pi=pi, n0=n0, nkc=nkc, bd_=bd_: e.matmul(
                        PS[bd_][:, n0:512], lhsT=ones_b, rhs=PTl[pi][:, n0:512], start=(kc == 0), stop=(kc == nkc - 1)),
                        reads=[B_cst, B_PTl[pi]], writes=[PSB[bd_]])
                    if kc == nkc - 1:
                        P.op('dve', lambda e, bd_=bd_, r0=r0, r1_=r1_: e.reciprocal(out=rden[r0:r1_, :], in_=PS[bd_][r0:r1_, :]),
                             reads=[PSB[bd_]], writes=[B_rden])
                        P.op('dve', lambda e, bo_=bo_, r0=r0, r1_=r1_, hp=hp, QB=QB: e.tensor_tensor(
                            out=foxT[r0:r1_, hp, QB * 512:(QB + 1) * 512], in0=PS[bo_][r0:r1_, :], in1=rden[r0:r1_, :], op=ALU.mult),
                            reads=[PSB[bo_], B_rden], writes=[B_fox])

                qproj(0)
                nt_ = len(tasks)
                for i in range(nt_ + LA):
                    if i < nt_:
                        st1(i)
                    if i >= LA:
                        st2(i - LA)
            W.release(slq)
            W.release(slk)
            for ti in range(NT):
                for half in range(2):
                    b_ = psrot.next()
                    for hp in range(4):
                        P.op('pe', lambda e, b_=b_, hp=hp, ti=ti, half=half: e.matmul(PS[b_][:], lhsT=foxT[:, hp, ti * 128:(ti + 1) * 128],
                                                                                    rhs=woF[:, hp, half * 512:(half + 1) * 512],
                                                                                    start=(hp == 0), stop=(hp == 3)),
                             reads=[B_fox, bwoF], writes=[PSB[b_]])
                    resid_add(ti, half, b_)
                if ti % 4 == 3:
                    hook(ti // 4)
            W.release(slf)

        def stage_mix1(hook, pre):
            if not pre:
                norm_to_hT('mix_o')
            fb = P.fence()
            yT = AR[:, 0:16624].rearrange("p (a b) -> p a b", a=8)
            sigt = [AR[:, 20592 + i * 512:20592 + (i + 1) * 512] for i in range(2)]
            B_y = Buf(fb); B_sig = [Buf(fb), Buf(fb)]
            P.op('pool', lambda e: e.memset(yT[:, :, 0:30], 0.0), writes=[B_y])
            sgr = Rot([0, 1])
            for hf in range(2):
                sla, wa, bwa = W.acquire(('cin', hf * 512))
                slg, wg, bwg = W.acquire(('cin', 1024 + hf * 512))
                for c4 in range(4):
                    cc = hf * 4 + c4
                    for tb in range(4):
                        ba, bb = psrot.next(), psrot.next()
                        for c in range(8):
                            P.op('pe', lambda e, ba=ba, c=c, c4=c4, tb=tb, wa=wa: e.matmul(
                                PS[ba][:], lhsT=wa[:, c, c4 * 128:(c4 + 1) * 128], rhs=hT[:, c, tb * 512:(tb + 1) * 512],
                                start=(c == 0), stop=(c == 7)), reads=[bwa, HB[tb]], writes=[PSB[ba]])
                        for c in range(8):
                            P.op('pe', lambda e, bb=bb, c=c, c4=c4, tb=tb, wg=wg: e.matmul(
                                PS[bb][:], lhsT=wg[:, c, c4 * 128:(c4 + 1) * 128], rhs=hT[:, c, tb * 512:(tb + 1) * 512],
                                start=(c == 0), stop=(c == 7)), reads=[bwg, HB[tb]], writes=[PSB[bb]])
                        si = sgr.next()
                        P.op('act', lambda e, bb=bb, si=si, cc=cc: e.activation(out=sigt[si], in_=PS[bb][:], func=AF.Sigmoid,
                                                                              bias=col('cbin', 8 + cc)),
                             reads=[PSB[bb], B_misc], writes=[B_sig[si]])
                        P.op('dve', lambda e, ba=ba, si=si, cc=cc, tb=tb: e.scalar_tensor_tensor(
                            out=yT[:, cc, 30 + tb * 512:30 + (tb + 1) * 512], in0=PS[ba][:], scalar=col('cbin', cc),
                            in1=sigt[si], op0=ALU.add, op1=ALU.mult), reads=[PSB[ba], B_sig[si], B_misc], writes=[B_y])
                W.release(sla)
                W.release(slg)
            c_sb = hT
            for cc in range(8):
                sld, diag, bdg = W.acquire(('diag', cc))
                for tb in range(4):
                    b_ = psrot.next()
                    for j in range(31):
                        P.op('pe', lambda e, b_=b_, j=j, cc=cc, tb=tb, diag=diag: e.matmul(
                            PS[b_][:], lhsT=diag[:, j, :], rhs=yT[:, cc, tb * 512 + j:tb * 512 + j + 512],
                            start=(j == 0), stop=(j == 30)), reads=[bdg, B_y], writes=[PSB[b_]])
                    P.op('act', lambda e, b_=b_, cc=cc, tb=tb: e.activation(out=c_sb[:, cc, tb * 512:(tb + 1) * 512], in_=PS[b_][:],
                                                                          func=AF.Identity, bias=col('dwb', cc)),
                         reads=[PSB[b_], B_misc], writes=[HB[tb]])
                W.release(sld)
            fb3 = P.fence()
            sqb = [AR[:, i * 4096:(i + 1) * 4096].rearrange("p (a b) -> p a b", a=8) for i in range(2)]
            nTb = [AR[:, 8192 + i * 4096:8192 + (i + 1) * 4096].rearrange("p (a b) -> p a b", a=8) for i in range(2)]
            tmpn = [AR[:, 16384 + i * 1024:16384 + (i + 1) * 1024].bitcast(F32) for i in range(2)]
            arf = [AR[:, 18432 + i * 1024:18432 + (i + 1) * 1024].bitcast(F32) for i in range(4)]
            B_sq = [Buf(fb3), Buf(fb3)]; B_nT = [Buf(fb3), Buf(fb3)]; B_fa = [Buf(fb3) for _ in range(4)]
            m_tb = [FA[:, i * 512:(i + 1) * 512] for i in range(4)]
            v_tb = [FA[:, 2048:2560], arf[0], arf[1], arf[2]]
            t1 = arf[3]
            B_t1 = Buf(fb3)
            B_tmp = [Buf(fb3), Buf(fb3)]
            slc = [W.acquire(('cout', 0)), W.acquire(('cout', 512))]
            tr = Rot([0, 1])

            def ln_stats(tb):
                k2 = tb % 2
                sq = sqb[k2]; m_t = m_tb[tb]; v_t = v_tb[tb]
                for cc in range(8):
                    P.op('act', lambda e, cc=cc, tb=tb, sq=sq: e.activation(out=sq[:, cc, :], in_=c_sb[:, cc, tb * 512:(tb + 1) * 512], func=AF.Square),
                         reads=[HB[tb]], writes=[B_sq[k2]])
                b1, b2 = psrot.next(), psrot.next()
                for cc in range(8):
                    P.op('pe', lambda e, b1=b1, cc=cc, tb=tb: e.matmul(PS[b1][:], lhsT=ones_b, rhs=c_sb[:, cc, tb * 512:(tb + 1) * 512],
                                                                     start=(cc == 0), stop=(cc == 7)), reads=[HB[tb], B_cst], writes=[PSB[b1]])
                for cc in range(8):
                    P.op('pe', lambda e, b2=b2, cc=cc, sq=sq: e.matmul(PS[b2][:], lhsT=ones_b, rhs=sq[:, cc, :],
                                                                     start=(cc == 0), stop=(cc == 7)), reads=[B_sq[k2], B_cst], writes=[PSB[b2]])
                P.op('act', lambda e, b1=b1, m_t=m_t: e.activation(out=m_t, in_=PS[b1][:], func=AF.Copy, scale=1.0 / D),
                     reads=[PSB[b1]], writes=[B_fa[tb]])
                P.op('dve', lambda e, m_t=m_t: e.tensor_tensor(out=t1, in0=m_t, in1=m_t, op=ALU.mult), reads=[B_fa[tb]], writes=[B_t1])
                P.op('dve', lambda e, b2=b2, v_t=v_t: e.scalar_tensor_tensor(out=v_t, in0=PS[b2][:], scalar=1.0 / D, in1=t1,
                                                                           op0=ALU.mult, op1=ALU.subtract),
                     reads=[PSB[b2], B_t1], writes=[B_fa[tb]])
                P.op('act', lambda e, v_t=v_t: e.activation(out=v_t, in_=v_t, func=AF.Ln, bias=col_eps()), reads=[B_fa[tb], B_cst], writes=[B_fa[tb]])
                P.op('act', lambda e, v_t=v_t: e.activation(out=v_t, in_=v_t, func=AF.Exp, scale=-0.5), reads=[B_fa[tb]], writes=[B_fa[tb]])

            def ln_apply(tb):
                k2 = tb % 2
                m_t = m_tb[tb]; v_t = v_tb[tb]; nTt = nTb[k2]
                for cc in range(8):
                    k = tr.next()
                    en = 'pool' if cc % 2 == 0 else 'dve'
                    P.op(en, lambda e, cc=cc, tb=tb, k=k, m_t=m_t: e.tensor_tensor(out=tmpn[k], in0=c_sb[:, cc, tb * 512:(tb + 1) * 512],
                                                                                 in1=m_t, op=ALU.subtract),
                         reads=[HB[tb], B_fa[tb]], writes=[B_tmp[k]])
                    P.op(en, lambda e, k=k, v_t=v_t: e.tensor_tensor(out=tmpn[k], in0=tmpn[k], in1=v_t, op=ALU.mult),
                         reads=[B_fa[tb]], writes=[B_tmp[k]])
                    P.op('act', lambda e, cc=cc, k=k, nTt=nTt: e.activation(out=nTt[:, cc, :], in_=tmpn[k], func=AF.Silu,
                                                                           scale=col('clng', cc), bias=col('clnb', cc)),
                         reads=[B_tmp[k], B_misc], writes=[B_nT[k2]])

            def out_proj(tb):
                k2 = tb % 2
                nTt = nTb[k2]
                for i in range(4):
                    ti = tb * 4 + i
                    for half in range(2):
                        _, wo_, bw = slc[half]
                        b_ = psrot.next()
                        P.op('pe', lambda e, b_=b_, half=half: e.matmul(PS[b_][:], lhsT=cst[0:1, 3, :], rhs=bout_b[0:1, half * 512:(half + 1) * 512],
                                                                      start=True, stop=False), reads=[B_cst, B_misc], writes=[PSB[b_]])
                        for cc in range(8):
                            P.op('pe', lambda e, b_=b_, cc=cc, i=i, wo_=wo_, nTt=nTt: e.matmul(PS[b_][:], lhsT=nTt[:, cc, i * 128:(i + 1) * 128],
                                                                                              rhs=wo_[:, cc, :], start=False, stop=(cc == 7)),
                                 reads=[bw, B_nT[k2]], writes=[PSB[b_]])
                        resid_add(ti, half, b_)

            for tb in range(4):
                ln_stats(tb)
            fb5 = P.fence()
            tmpn = [AR[:, i * 1024:(i + 1) * 1024].bitcast(F32) for i in range(4)]
            B_tmp = [Buf(fb5) for _ in range(4)]
            tr = Rot(range(4))
            for tb in range(4):
                ln_apply(tb)
                if tb > 0:
                    out_proj(tb - 1)
                    hook(tb - 1)
            out_proj(3)
            hook(3)
            for sl, _, _ in slc:
                W.release(sl)

        d_out = [P.dma_sem() for _ in range(NT)]
        B_outd = Buf()

        stg = [AR[:, 9216 + k * 2048:9216 + (k + 1) * 2048].bitcast(F32) for k in range(6)]
        B_stg = [None] * 6

        use_stg = stages[-1].startswith('ffn')

        def out_group(s, tb):
            if not use_stg:
                if final_norm:
                    norm_stats(lambda i, tb=tb: x_sb[:, tb * 4 + i, :], 4, XB[tb * 4:tb * 4 + 4], tb * 4)
                for i in range(4):
                    ti = tb * 4 + i
                    if final_norm:
                        P.op('dve', lambda e, ti=ti: e.scalar_tensor_tensor(out=x_sb[:, ti, :], in0=x_sb[:, ti, :], scalar=rstd[:, ti:ti + 1],
                                                                           in1=rows[:, 0:1024], op0=ALU.mult, op1=ALU.mult),
                             reads=[B_small, B_misc], writes=[XB[ti]])
                    P.dma('sp', d_out[ti], out_d[s, ti * 128:(ti + 1) * 128, :], x_sb[:, ti, :], reads=[XB[ti]], writes=[B_outd])
                    if s + 1 < NSEQ:
                        P.dma('sp', d_x[ti], x_sb[:, ti, :], x_d[s + 1, ti * 128:(ti + 1) * 128, :], writes=[XB[ti]])
                return
            if tb == 0:
                fbo = P.fence()
                for k in range(6):
                    B_stg[k] = Buf(fbo)
            if final_norm:
                norm_stats(lambda i, tb=tb: x_sb[:, tb * 4 + i, :], 4, XB[tb * 4:tb * 4 + 4], tb * 4)
            for i in range(4):
                ti = tb * 4 + i
                k = ti % 6
                if final_norm:
                    P.op('dve', lambda e, ti=ti, k=k: e.scalar_tensor_tensor(out=stg[k], in0=x_sb[:, ti, :], scalar=rstd[:, ti:ti + 1],
                                                                            in1=rows[:, 0:1024], op0=ALU.mult, op1=ALU.mult),
                         reads=[XB[ti], B_small, B_misc], writes=[B_stg[k]])
                else:
                    P.op('dve', lambda e, ti=ti, k=k: e.tensor_copy(out=stg[k], in_=x_sb[:, ti, :]), reads=[XB[ti]], writes=[B_stg[k]])
                if s + 1 < NSEQ:
                    P.dma('sp', d_x[ti], x_sb[:, ti, :], x_d[s + 1, ti * 128:(ti + 1) * 128, :], writes=[XB[ti]])
                P.dma('sp', d_out[ti], out_d[s, ti * 128:(ti + 1) * 128, :], stg[k], reads=[B_stg[k]], writes=[B_outd])

        def load_x(s):
            for ti in range(NT):
                P.dma('sp', d_x[ti], x_sb[:, ti, :], x_d[s, ti * 128:(ti + 1) * 128, :], writes=[XB[ti]])

        cast_order = []
        for st in stages:
            cast_order += {'mix0': ['w_in_e', 'w_out_e'], 'xa0': ['wkv0', 'wq0', 'wo0'], 'ffn0': ['wgu0', 'wdn0'],
                           'mix1': ['conv_w_in', 'conv_w_out'], 'xa1': ['wkv1', 'wq1', 'wo1'], 'ffn1': ['wgu1', 'wdn1']}[st]
        do_casts(cast_order)
        if 'mix1' in stages:
            B_db = Buf()
            dstage = AR[:, 0:3968].rearrange("p (a b) -> p a b", a=31)
            for cc in range(8):
                for j in range(31):
                    P.op('pool', lambda e, j=j, cc=cc: e.tensor_scalar(out=dstage[:, j, :], in0=ident, scalar1=col('dww', cc * 31 + j),
                                                                     scalar2=0.0, op0=ALU.mult, op1=ALU.add),
                         reads=[B_cst, B_misc], writes=[B_db])
                kd = P.dma_sem()
                WB['diag%d' % cc] = Buf()
                P.dma('sp', kd, s_diag[cc], AR[:, 0:3968], reads=[B_db], writes=[WB['diag%d' % cc]])
        load_x(0)
        W.start()
        kv_prep(0)
        NORM_NAME = {'mix0': 'mix_e', 'xa0': 'xa0', 'ffn0': 'ffn0', 'mix1': 'mix_o', 'xa1': 'xa1', 'ffn1': 'ffn1'}
        for s in range(NSEQ):
            for si, st in enumerate(stages):
                if si + 1 < len(stages):
                    hook = (lambda tb, g=NORM_NAME[stages[si + 1]]: norm_hook(g, tb))
                else:
                    hook = (lambda tb, s=s: out_group(s, tb))
                pre = si > 0
                P.tag = '%d:%s' % (s, st)
                if st == 'ffn0':
                    if st == stages[-1] and SPLIT_LAST:
                        stage_ffn(0, hook, pre, s, (0, 1))
                        stage_ffn(0, hook, True, s, (2, 3))
                    else:
                        stage_ffn(0, hook, pre, s)
                elif st == 'ffn1':
                    if st == stages[-1] and SPLIT_LAST:
                        stage_ffn(1, hook, pre, s, (0, 1))
                        stage_ffn(1, hook, True, s, (2, 3))
                    else:
                        stage_ffn(1, hook, pre, s)
                elif st == 'xa0':
                    stage_xa(0, s, hook, pre)
                elif st == 'xa1':
                    stage_xa(1, s, hook, pre)
                elif st == 'mix0':
                    stage_mix0(s, hook, pre)
                elif st == 'mix1':
                    stage_mix1(hook, pre)
            if s + 1 < NSEQ:
                P.tag = '%d:kv' % (s + 1)
                kv_prep(s + 1)
        P.final_wait('sp', [B_outd])
        P.emit()
    return nc


def host_prep(inp):
    f = lambda a: np.ascontiguousarray(np.asarray(a, dtype=np.float32))
    cols = np.zeros((128, NCOL), np.float32)

    def put(name, vec):
        c0, n = COLS[name]
        cols[:, c0:c0 + n] = f(vec).reshape(n, 128).T

    put('mix_e', inp['mix_norm_e'][0]); put('xa0', inp['xa_norm'][0]); put('ffn0', inp['ffn_norm'][0])
    put('mix_o', inp['mix_norm_o'][0]); put('xa1', inp['xa_norm'][1]); put('ffn1', inp['ffn_norm'][1])
    put('mem0', inp['mem_norm'][0]); put('mem1', inp['mem_norm'][1])
    put('cbin', inp['conv_b_in'][0]); put('dwb', inp['conv_dw_b'][0])
    put('clng', inp['conv_ln_g'][0]); put('clnb', inp['conv_ln_b'][0])
    c0, n = COLS['dww']
    cols[:, c0:c0 + n] = f(inp['conv_dw_w'][0]).reshape(31, 8, 128).transpose(2, 1, 0).reshape(128, 248)
    c0, n = COLS['gbs']
    cols[:, c0:c0 + n] = f(inp['gmlp_b_s'][0]).T
    c0, n = COLS['fbias']
    cols[0:8, c0] = f(inp['fox_f_bias'][0])
    rows = np.zeros((128, 2176), np.float32)
    rows[:, 0:1024] = f(inp['final_norm'])[None, :]
    rows[:, 1024:1536] = f(inp['gmlp_ln_g'][0])[None, :]
    rows[:, 1536:2048] = f(inp['gmlp_ln_b'][0])[None, :]
    rows[:, 2048:2176] = np.tile(f(inp['fox_f_bias'][0]), 16)[None, :]
    wsT = np.ascontiguousarray(f(inp['gmlp_w_s'][0]).transpose(2, 0, 1))
    cst = np.zeros((128, 4, 128), np.float32)
    idx = np.arange(128)
    cst[:, 0, :] = np.eye(128, dtype=np.float32)
    cst[:, 1, :] = np.where(idx[None, :] >= idx[:, None], 0.0, -30000.0)
    cst[:, 2, :] = (idx[None, :] >= idx[:, None]).astype(np.float32)
    cst[:, 3, :] = 1.0
    c32 = np.zeros((128, 3, 128), np.float32)
    c32[:, 0, :] = (idx[None, :] >= idx[:, None]).astype(np.float32)
    c32[:, 1, :] = 1.0
    jj, hh = idx // 8, idx % 8
    c32[:, 2, :] = ((hh[:, None] == hh[None, :]) & (jj[:, None] < jj[None, :])).astype(np.float32)
    shared = {
        'w_in_e': f(inp['w_in_e'][0]), 'w_out_e': f(inp['w_out_e'][0]),
        'conv_w_in': f(inp['conv_w_in'][0]), 'conv_w_out': f(inp['conv_w_out'][0]),
        'xa_wq': f(inp['xa_wq']), 'xa_wkv': f(inp['xa_wkv']), 'xa_wo': f(inp['xa_wo']),
        'ffn_w_gu': f(inp['ffn_w_gu']), 'ffn_w_down': f(inp['ffn_w_down']),
        'cols': cols, 'rows': rows, 'bout': f(inp['conv_b_out'][0]).reshape(1, D), 'wsT': wsT, 'cst': cst, 'c32': c32,
    }
    return shared


LAST_PROG = None
ALL_STAGES = ('mix0', 'xa0', 'ffn0', 'mix1', 'xa1', 'ffn1')
_NC_CACHE = {}


def kernel(**inputs):
    x = np.asarray(inputs['x'], dtype=np.float32)
    mem = np.asarray(inputs['mem'], dtype=np.float32)
    shared = host_prep(inputs)
    nseq = x.shape[0] // NCORES
    key = (nseq, ALL_STAGES)
    if key not in _NC_CACHE:
        _NC_CACHE[key] = build_program(nseq, ALL_STAGES)
    nc = _NC_CACHE[key]
    in_maps = []
    for c in range(NCORES):
        m = dict(shared)
        m['x'] = np.ascontiguousarray(x[c * nseq:(c + 1) * nseq])
        m['mem'] = np.ascontiguousarray(mem[c * nseq:(c + 1) * nseq])
        in_maps.append(m)
    res = run_bass_kernel_spmd(nc, in_maps, core_ids=list(range(NCORES)))
    return np.concatenate([r['out'] for r in res.results], axis=0)
```

```python
import numpy as np
from contextlib import ExitStack
import concourse.bass as bass
import concourse.mybir as mybir
from concourse.bass_utils import run_bass_kernel_spmd

F32 = mybir.dt.float32
BF16 = mybir.dt.bfloat16
AF = mybir.ActivationFunctionType
ALU = mybir.AluOpType

ENG = ['pe', 'act', 'dve', 'pool', 'sp']
T = 2048
D = 1024
NT = 16
EPS = 1e-6
FFN_H = 2816
NCORES = 8


class Buf:
    __slots__ = ('w', 'r')

    def __init__(self, src=None):
        self.w = dict(src.w) if src is not None else {}
        self.r = dict(src.r) if src is not None else {}


class Prog:
    def __init__(self, nc, es):
        self.nc = nc
        self.tag = ''
        self.tags = {e: [] for e in ENG}
        self.ops = {e: [] for e in ENG}
        self.cnt = {e: 0 for e in ENG}
        self.seen = {e: {} for e in ENG}
        self.sems = {}
        self.dma_cnt = {}
        self.es = es
        for e in ENG:
            if e != 'sp':
                self.sems[e] = es.enter_context(nc.semaphore('s_' + e))
        self.n_dma = 0

    def dma_sem(self):
        k = ('dma', self.n_dma)
        self.n_dma += 1
        self.sems[k] = self.es.enter_context(self.nc.semaphore('d%d' % k[1]))
        self.dma_cnt[k] = 0
        return k

    def _waits(self, e, deps):
        best = {}
        for k, v in deps:
            if e == 'pe' and k == 'pe':
                continue
            if v > best.get(k, 0):
                best[k] = v
        out = []
        for k, v in best.items():
            if v > self.seen[e].get(k, 0):
                self.seen[e][k] = v
                out.append((k, v))
        return out

    @staticmethod
    def _deps(reads, writes):
        deps = []
        for b in reads:
            deps.extend(b.w.items())
        for b in writes:
            deps.extend(b.w.items())
            deps.extend(b.r.items())
        return deps

    @staticmethod
    def _mark(reads, writes, k, v):
        for b in reads:
            if v > b.r.get(k, 0):
                b.r[k] = v
        for b in writes:
            if v > b.w.get(k, 0):
                b.w[k] = v

    def op(self, e, fn, reads=(), writes=()):
        waits = self._waits(e, self._deps(reads, writes))
        self.cnt[e] += 1
        self.tags[e].append((self.tag, tuple(waits)))
        self.ops[e].append((waits, fn, e))
        self._mark(reads, writes, e, self.cnt[e])

    def dma(self, q, semk, out, in_, reads=(), writes=(), **kw):
        waits = self._waits(q, self._deps(reads, writes))
        self.dma_cnt[semk] += 16
        v = self.dma_cnt[semk]
        self.ops[q].append((waits, lambda eng, o=out, i=in_, kw=kw: eng.dma_start(out=o, in_=i, **kw), semk))
        self._mark(reads, writes, semk, v)

    def fence(self, engines_only=False):
        b = Buf()
        for e in ENG:
            if e != 'sp' and self.cnt[e] > 0:
                b.w[e] = self.cnt[e]
        if engines_only:
            return b
        for k, v in self.dma_cnt.items():
            if v > 0:
                b.w[k] = v
        return b

    def final_wait(self, e, bufs):
        deps = []
        for b in bufs:
            deps.extend(b.w.items())
            deps.extend(b.r.items())
        self.ops[e].append((self._waits(e, deps), None, None))

    def emit(self):
        prog = self
        with self.nc.Block() as block:
            def run(ename):
                def body(eng):
                    for waits, fn, inc in prog.ops[ename]:
                        for k, v in waits:
                            eng.wait_ge(prog.sems[k], v)
                        if fn is None:
                            continue
                        ins = fn(eng)
                        if isinstance(inc, tuple):
                            ins.then_inc(prog.sems[inc], 16)
                        else:
                            ins.then_inc(prog.sems[inc], 1)
                return body
            block.tensor(run('pe'))
            block.scalar(run('act'))
            block.vector(run('dve'))
            block.gpsimd(run('pool'))
            block.sync(run('sp'))


class Rot:
    def __init__(self, items):
        self.items = list(items)
        self.i = 0

    def next(self):
        it = self.items[self.i % len(self.items)]
        self.i += 1
        return it


COLS = {}
_c = 0
for _name, _n in [('mix_e', 8), ('xa0', 8), ('ffn0', 8), ('mix_o', 8), ('xa1', 8), ('ffn1', 8),
                  ('mem0', 8), ('mem1', 8), ('cbin', 16), ('dwb', 8), ('clng', 8), ('clnb', 8),
                  ('dww', 248), ('gbs', 8), ('fbias', 1)]:
    COLS[_name] = (_c, _n)
    _c += _n
NCOL = _c


def build_program(NSEQ, stages, final_norm=True):
    SPLIT_LAST = False
    nc = bass.Bass("TRN2", target_bir_lowering=False)

    def din(name, shape, dt=F32):
        return nc.dram_tensor(name, list(shape), dt, kind="ExternalInput").ap()

    x_d = din("x", [NSEQ, T, D])
    mem_d = din("mem", [NSEQ, 256, D])
    out_d = nc.dram_tensor("out", [NSEQ, T, D], F32, kind="ExternalOutput").ap()
    w_in_e = din("w_in_e", [D, 2568])
    w_out_e = din("w_out_e", [D, D])
    conv_w_in = din("conv_w_in", [D, 2048])
    conv_w_out = din("conv_w_out", [D, D])
    xa_wq = din("xa_wq", [2, D, D])
    xa_wkv = din("xa_wkv", [2, D, 2048])
    xa_wo = din("xa_wo", [2, D, D])
    ffn_w_gu = din("ffn_w_gu", [2, D, 2 * FFN_H])
    ffn_w_down = din("ffn_w_down", [2, FFN_H, D])
    cols_d = din("cols", [128, NCOL])
    rows_d = din("rows", [128, 2176])
    bout_d = din("bout", [1, D])
    wsT_d = din("wsT", [128, 8, 128])
    cst_d = din("cst", [128, 4, 128])
    c32_d = din("c32", [128, 3, 128])

    def dscr(name, shape):
        return nc.dram_tensor(name, list(shape), BF16, kind="Internal").ap()

    s_w_in_e = dscr("s_w_in_e", [128, 8 * 2568])
    s_w_out_e = dscr("s_w_out_e", [128, 8 * 1024])
    s_conv_w_in = dscr("s_conv_w_in", [128, 8 * 2048])
    s_conv_w_out = dscr("s_conv_w_out", [128, 8 * 1024])
    cumd = dscr("cumd", [128, 6 * 128])
    s_diag = [dscr("s_diag%d" % cc, [128, 31 * 128]) for cc in range(8)]
    kv_scr = [dscr("kv_scr%d" % l, [128, 4096]) for l in range(2)]
    s_wq = [dscr("s_wq%d" % l, [128, 8 * 1024]) for l in range(2)]
    s_wkv = [dscr("s_wkv%d" % l, [128, 8 * 2048]) for l in range(2)]
    s_wo = [dscr("s_wo%d" % l, [128, 8 * 1024]) for l in range(2)]
    s_wgu = [dscr("s_wgu%d" % l, [128, 8 * 5632]) for l in range(2)]
    s_wdn = [dscr("s_wdn%d" % l, [128, 22 * 1024]) for l in range(2)]

    es = ExitStack()
    with es:
        def sb(name, shape, dt):
            return es.enter_context(nc.sbuf_tensor(name, list(shape), dt))

        P = Prog(nc, es)
        global LAST_PROG
        LAST_PROG = P
        x_sb = sb("x_sb", [128, NT, D], F32)
        hT = sb("hT", [128, 8, T], BF16)
        AR = sb("AR", [128, 23552], BF16)
        FA = sb("FA", [128, 2560], F32)
        WS = [sb("ws%d" % i, [128, 4096], BF16) for i in range(4)]
        xn = [sb("xn%d" % i, [128, D], BF16) for i in range(2)]
        junk = sb("junk", [128, D], BF16)
        rows = sb("rows_sb", [128, 2176], F32)
        cols = sb("cols_sb", [128, NCOL], F32)
        cst = sb("cst_sb", [128, 4, 128], BF16)
        wmT = sb("wmT", [128, 8, 128], BF16)
        c32 = sb("c32_sb", [128, 3, 128], F32)
        wf = sb("wf", [128, 8, 8], BF16)
        bout_b = sb("bout_b", [1, D], BF16)
        ss = sb("ss", [128, 32], F32)
        rstd = sb("rstd", [128, 32], F32)
        small = sb("small", [128, 64], F32)
        PS = [es.enter_context(nc.psum_tensor("ps%d" % i, [128, 512], F32)) for i in range(8)]
        PSB = [Buf() for _ in range(8)]

        wsT_f = FA[:, 0:1024].rearrange("p (a b) -> p a b", a=8)
        cstf = FA[:, 1024:1536].rearrange("p (a b) -> p a b", a=4)
        ident = cst[:, 0, :]
        maskadd = cst[:, 1, :]
        ones_b = cst[:, 3, :]

        def col(name, j=0, n=1, p0=0, p1=128):
            c0, _ = COLS[name]
            return cols[p0:p1, c0 + j:c0 + j + n]

        d_misc = P.dma_sem()
        B_misc = Buf()
        d_misc2 = P.dma_sem()
        P.dma('pool', d_misc2, bout_b[:], bout_d, writes=[B_misc])
        P.dma('pool', d_misc2, wf[:], w_in_e.rearrange("(c p) n -> p c n", p=128)[:, :, 1536:1544], writes=[B_misc])
        for dst, src in [(c32[:], c32_d), (cols[:], cols_d), (rows[:], rows_d), (cstf, cst_d), (wsT_f, wsT_d),
                         ]:
            P.dma('sp', d_misc, dst, src, writes=[B_misc])
        B_cst = Buf()
        P.op('dve', lambda e: e.tensor_copy(out=cst[:], in_=cstf), reads=[B_misc], writes=[B_cst])
        for g in range(8):
            P.op('dve', lambda e, g=g: e.tensor_tensor(out=wmT[:, g, :], in0=wsT_f[:, g, :], in1=cstf[:, 2, :], op=ALU.mult),
                 reads=[B_misc], writes=[B_cst])

        WB = {}

        def cast(name, dst, src3, ncol, nchunk=8):
            k = P.dma_sem()
            b = Buf()
            d3 = dst.rearrange("p (c n) -> p c n", c=nchunk)
            for c0 in range(0, nchunk, 8):
                c1 = min(nchunk, c0 + 8)
                for n0 in range(0, ncol, 2048):
                    n1 = min(ncol, n0 + 2048)
                    P.dma('pool', k, d3[:, c0:c1, n0:n1], src3[:, c0:c1, n0:n1], writes=[b])
            WB[name] = b

        def kview(w):
            return w.rearrange("(c p) n -> p c n", p=128)

        def do_casts(which):
            for name in which:
                if name == 'w_in_e':
                    cast(name, s_w_in_e, kview(w_in_e), 2568)
                elif name == 'w_out_e':
                    cast(name, s_w_out_e, kview(w_out_e), 1024)
                elif name == 'conv_w_in':
                    cast(name, s_conv_w_in, kview(conv_w_in), 2048)
                elif name == 'conv_w_out':
                    cast(name, s_conv_w_out, kview(conv_w_out), 1024)
                elif name[:2] == 'wq':
                    l = int(name[2]); cast(name, s_wq[l], kview(xa_wq[l]), 1024)
                elif name[:3] == 'wkv':
                    l = int(name[3]); cast(name, s_wkv[l], kview(xa_wkv[l]), 2048)
                elif name[:2] == 'wo':
                    l = int(name[2]); cast(name, s_wo[l], kview(xa_wo[l]), 1024)
                elif name[:3] == 'wgu':
                    l = int(name[3]); cast(name, s_wgu[l], kview(ffn_w_gu[l]), 5632)
                elif name[:3] == 'wdn':
                    l = int(name[3]); cast(name, s_wdn[l], kview(ffn_w_down[l]), 1024, nchunk=22)

        def piece_src(spec):
            kind = spec[0]
            if kind == 'in_e':
                return 'w_in_e', s_w_in_e.rearrange("p (c n) -> p c n", c=8)[:, :, spec[1]:spec[1] + 512], (8, 512)
            if kind == 'out_e':
                return 'w_out_e', s_w_out_e.rearrange("p (c n) -> p c n", c=8)[:, spec[1]:spec[1] + 4, :], (4, 1024)
            if kind == 'cin':
                return 'conv_w_in', s_conv_w_in.rearrange("p (c n) -> p c n", c=8)[:, :, spec[1]:spec[1] + 512], (8, 512)
            if kind == 'cout':
                return 'conv_w_out', s_conv_w_out.rearrange("p (c n) -> p c n", c=8)[:, :, spec[1]:spec[1] + 512], (8, 512)
            if kind == 'wq':
                return 'wq%d' % spec[1], s_wq[spec[1]].rearrange("p (c n) -> p c n", c=8)[:, :, spec[2]:spec[2] + 512], (8, 512)
            if kind == 'wkv':
                return 'wkv%d' % spec[1], s_wkv[spec[1]].rearrange("p (c n) -> p c n", c=8)[:, :, spec[2]:spec[2] + 512], (8, 512)
            if kind == 'wo':
                return 'wo%d' % spec[1], s_wo[spec[1]].rearrange("p (c n) -> p c n", c=8)[:, :, spec[2]:spec[2] + 512], (8, 512)
            if kind == 'wgu':
                n = spec[3]
                return 'wgu%d' % spec[1], s_wgu[spec[1]].rearrange("p (c n) -> p c n", c=8)[:, :, spec[2]:spec[2] + n], (8, n)
            if kind == 'wdn':
                n = spec[3]
                return 'wdn%d' % spec[1], s_wdn[spec[1]].rearrange("p (c n) -> p c n", c=22)[:, spec[2]:spec[2] + n, :], (n, 1024)
            if kind == 'diag':
                return 'diag%d' % spec[1], s_diag[spec[1]].rearrange("p (a b) -> p a b", a=31), (31, 128)
            raise ValueError(spec)

        def seq_pieces():
            pl = []
            for st in stages:
                if st == 'mix0':
                    pl += [('in_e', 1544), ('in_e', 2056), ('out_e', 4), ('in_e', 1024), ('in_e', 0), ('in_e', 512), ('out_e', 0)]
                elif st in ('xa0', 'xa1'):
                    l = int(st[2])
                    pl += [('wq', l, 0), ('wq', l, 512), ('wo', l, 0), ('wo', l, 512)]
                elif st in ('ffn0', 'ffn1'):
                    l = int(st[3])
                    for gi in list(range(6)) * (2 if (st == stages[-1] and SPLIT_LAST) else 1):
                        nf = 4 if gi < 5 else 2
                        pl += [('wgu', l, gi * 512, nf * 128), ('wgu', l, FFN_H + gi * 512, nf * 128), ('wdn', l, gi * 4, nf)]
                elif st == 'mix1':
                    pl += [('cin', 0), ('cin', 1024), ('cin', 512), ('cin', 1536)] + [('diag', cc) for cc in range(8)] + [('cout', 0), ('cout', 512)]
            return pl

        kv_pieces = []
        for st in stages:
            if st[:2] == 'xa':
                l = int(st[2])
                kv_pieces += [('wkv', l, 0), ('wkv', l, 512), ('wkv', l, 1024), ('wkv', l, 1536)]
        all_pieces = list(kv_pieces)
        for s in range(NSEQ):
            all_pieces += seq_pieces()
            if s + 1 < NSEQ:
                all_pieces += kv_pieces

        class WStream:
            def __init__(self):
                self.next_load = 0
                self.next_acq = 0
                self.slot_of = {}
                self.sbuf = [Buf() for _ in WS]
                self.ssem = [P.dma_sem() for _ in WS]

            def _load(self, slot):
                if self.next_load >= len(all_pieces):
                    return
                idx = self.next_load
                self.next_load += 1
                name, src, (a, b) = piece_src(all_pieces[idx])
                dst = WS[slot][:, 0:a * b].rearrange("p (a b) -> p a b", a=a)
                P.dma('sp', self.ssem[slot], dst, src, reads=[WB[name]], writes=[self.sbuf[slot]])
                self.slot_of[idx] = slot

            def start(self):
                for s in range(len(WS)):
                    self._load(s)

            def acquire(self, spec):
                idx = self.next_acq
                assert all_pieces[idx] == spec, (all_pieces[idx], spec)
                assert idx in self.slot_of, "weight slot deadlock at piece %d %s" % (idx, spec)
                self.next_acq += 1
                slot = self.slot_of[idx]
                _, _, (a, b) = piece_src(spec)
                return slot, WS[slot][:, 0:a * b].rearrange("p (a b) -> p a b", a=a), self.sbuf[slot]

            def release(self, slot):
                self._load(slot)

        psrot = Rot(range(8))
        d_x = [P.dma_sem() for _ in range(NT)]
        XB = [Buf() for _ in range(NT)]
        HB = [Buf() for _ in range(4)]
        B_xn = [Buf(), Buf()]
        B_small = Buf()
        xn_rot = Rot([0, 1])

        def norm_stats(tiles_ap_fn, n, bufs, ss_off):
            for i in range(n):
                P.op('act', lambda e, i=i: e.activation(out=junk[:], in_=tiles_ap_fn(i), func=AF.Square,
                                                        accum_out=ss[:, ss_off + i:ss_off + i + 1]),
                     reads=[bufs[i]], writes=[B_small])
            P.op('act', lambda e: e.activation(out=ss[:, ss_off:ss_off + n], in_=ss[:, ss_off:ss_off + n], func=AF.Sqrt,
                                               scale=1.0 / D, bias=col_eps()),
                 reads=[B_small, B_cst], writes=[B_small])
            P.op('dve', lambda e: e.reciprocal(out=rstd[:, ss_off:ss_off + n], in_=ss[:, ss_off:ss_off + n]),
                 reads=[B_small], writes=[B_small])

        eps_t = sb("eps_t", [128, 1], F32)
        P.op('dve', lambda e: e.memset(eps_t[:], EPS), writes=[B_cst])

        def col_eps():
            return eps_t[:, 0:1]

        def norm_to_hT(gname):
            for tb in range(4):
                norm_group(gname, tb)

        def norm_group(gname, tb):
            norm_group_stats(tb)
            norm_group_apply(gname, tb)

        def norm_group_stats(tb):
            norm_stats(lambda i, tb=tb: x_sb[:, tb * 4 + i, :], 4, XB[tb * 4:tb * 4 + 4], tb * 4)

        def norm_hook(gname, tb):
            norm_group_stats(tb)
            if tb >= 1:
                norm_group_apply(gname, tb - 1)
            if tb == 3:
                norm_group_apply(gname, 3)

        def norm_group_apply(gname, tb):
            if True:
                banks = [psrot.next() for _ in range(4)]
                for i in range(4):
                    ti = tb * 4 + i
                    xi = xn_rot.next()
                    P.op('pool', lambda e, ti=ti, xi=xi: e.tensor_scalar(out=xn[xi][:], in0=x_sb[:, ti, :],
                                                                        scalar1=rstd[:, ti:ti + 1], scalar2=0.0,
                                                                        op0=ALU.mult, op1=ALU.add),
                         reads=[XB[ti], B_small], writes=[B_xn[xi]])
                    for c in range(8):
                        bk = banks[c // 2]
                        o = PS[bk][:].bitcast(BF16)[:, (c % 2) * 512 + i * 128:(c % 2) * 512 + (i + 1) * 128]
                        P.op('pe', lambda e, o=o, xi=xi, c=c: e.transpose(out=o, in_=xn[xi][:, c * 128:(c + 1) * 128], identity=ident),
                             reads=[B_xn[xi], B_cst], writes=[PSB[bk]])
                for c in range(8):
                    bk = banks[c // 2]
                    src = PS[bk][:].bitcast(BF16)[:, (c % 2) * 512:(c % 2) * 512 + 512]
                    eng = 'act' if c % 2 == 0 else 'dve'
                    if eng == 'act':
                        P.op('act', lambda e, src=src, c=c, tb=tb: e.activation(out=hT[:, c, tb * 512:(tb + 1) * 512], in_=src,
                                                                               func=AF.Identity, scale=col(gname, c)),
                             reads=[PSB[bk], B_misc], writes=[HB[tb]])
                    else:
                        P.op('dve', lambda e, src=src, c=c, tb=tb: e.tensor_scalar(out=hT[:, c, tb * 512:(tb + 1) * 512], in0=src,
                                                                                  scalar1=col(gname, c), scalar2=None, op0=ALU.mult),
                             reads=[PSB[bk], B_misc], writes=[HB[tb]])

        def resid_add(ti, half, bk):
            P.op('dve', lambda e: e.tensor_tensor(out=x_sb[:, ti, half * 512:(half + 1) * 512], in0=PS[bk][:],
                                                  in1=x_sb[:, ti, half * 512:(half + 1) * 512], op=ALU.add),
                 reads=[PSB[bk]], writes=[XB[ti]])

        W = WStream()

        def stage_ffn(l, hook, pre, s=0, tbs=(0, 1, 2, 3)):
            if not pre:
                norm_to_hT('ffn%d' % l)
            if stages[-1] == 'ffn%d' % l and s + 1 < NSEQ:
                kv_mem_load(s + 1)
            fb = P.fence()
            B_hid = Buf(fb)
            B_sg = [Buf(fb), Buf(fb)]
            hid = AR[:, 0:8192].rearrange("p (a b) -> p a b", a=4)
            sg = [AR[:, 8192 + i * 512:8192 + (i + 1) * 512] for i in range(2)]
            sgr = Rot([0, 1])
            for gi in range(6):
                nf = 4 if gi < 5 else 2
                sl_g, wg, bg = W.acquire(('wgu', l, gi * 512, nf * 128))
                sl_u, wu, bu = W.acquire(('wgu', l, FFN_H + gi * 512, nf * 128))
                sl_d, wd, bd = W.acquire(('wdn', l, gi * 4, nf))
                for jj in range(nf):
                    for tb in tbs:
                        ba, bb = psrot.next(), psrot.next()
                        for c in range(8):
                            P.op('pe', lambda e, ba=ba, c=c, jj=jj, tb=tb, wg=wg: e.matmul(
                                PS[ba][:], lhsT=wg[:, c, jj * 128:(jj + 1) * 128], rhs=hT[:, c, tb * 512:(tb + 1) * 512],
                                start=(c == 0), stop=(c == 7)), reads=[bg, HB[tb]], writes=[PSB[ba]])
                        for c in range(8):
                            P.op('pe', lambda e, bb=bb, c=c, jj=jj, tb=tb, wu=wu: e.matmul(
                                PS[bb][:], lhsT=wu[:, c, jj * 128:(jj + 1) * 128], rhs=hT[:, c, tb * 512:(tb + 1) * 512],
                                start=(c == 0), stop=(c == 7)), reads=[bu, HB[tb]], writes=[PSB[bb]])
                        si = sgr.next()
                        P.op('act', lambda e, ba=ba, si=si: e.activation(out=sg[si], in_=PS[ba][:], func=AF.Silu),
                             reads=[PSB[ba]], writes=[B_sg[si]])
                        P.op('dve', lambda e, bb=bb, si=si, jj=jj, tb=tb: e.tensor_tensor(
                            out=hid[:, jj, tb * 512:(tb + 1) * 512], in0=PS[bb][:], in1=sg[si], op=ALU.mult),
                            reads=[PSB[bb], B_sg[si]], writes=[B_hid])
                W.release(sl_g)
                W.release(sl_u)
                for ti in range(tbs[0] * 4, tbs[-1] * 4 + 4):
                    for half in range(2):
                        bk = psrot.next()
                        for jj in range(nf):
                            P.op('pe', lambda e, bk=bk, jj=jj, ti=ti, half=half, wd=wd: e.matmul(
                                PS[bk][:], lhsT=hid[:, jj, ti * 128:(ti + 1) * 128], rhs=wd[:, jj, half * 512:(half + 1) * 512],
                                start=(jj == 0), stop=(jj == nf - 1)), reads=[bd, B_hid], writes=[PSB[bk]])
                        resid_add(ti, half, bk)
                    if gi == 5 and ti % 4 == 3:
                        hook(ti // 4)
                W.release(sl_d)

        d_mem = P.dma_sem()
        B_kvs = [Buf(), Buf()]
        d_kvs = [P.dma_sem(), P.dma_sem()]
        d_kvl = P.dma_sem()
        xa_layers = [int(st[2]) for st in stages if st[:2] == 'xa']

        memH = {}

        def kv_mem_load(s):
            if not xa_layers or s in memH:
                return
            memH[s] = Buf(P.fence())
            memt = FA[:, 0:2048].rearrange("p (a b) -> p a b", a=2)
            P.dma('sp', d_mem, memt, mem_d[s].rearrange("(a p) d -> p a d", p=128), writes=[memH[s]])

        def kv_prep(s):
            if not xa_layers:
                return
            kv_mem_load(s)
            fb = P.fence(engines_only=(s > 0))
            B_mem = memH[s]; B_mT = Buf(fb); B_kT = Buf(fb); B_V = Buf(fb)
            memt = FA[:, 0:2048].rearrange("p (a b) -> p a b", a=2)
            mT = AR[:, 0:2048].rearrange("p (a b) -> p a b", a=8)
            kvt = AR[:, 2048:6144]
            kT = AR[:, 2048:4096].rearrange("p (a b) -> p a b", a=8)
            Vt = AR[:, 4096:6144].rearrange("p (a b) -> p a b", a=2)
            norm_stats(lambda i: memt[:, i, :], 2, [B_mem, B_mem], 16)
            for l in xa_layers:
                bk = psrot.next()
                bk2 = psrot.next()
                for i in range(2):
                    xi = xn_rot.next()
                    P.op('pool', lambda e, i=i, xi=xi: e.tensor_scalar(out=xn[xi][:], in0=memt[:, i, :], scalar1=rstd[:, 16 + i:17 + i],
                                                                      scalar2=0.0, op0=ALU.mult, op1=ALU.add),
                         reads=[B_mem, B_small], writes=[B_xn[xi]])
                    for c in range(8):
                        b_ = bk if c < 4 else bk2
                        o = PS[b_][:].bitcast(BF16)[:, (c % 4) * 256 + i * 128:(c % 4) * 256 + (i + 1) * 128]
                        P.op('pe', lambda e, o=o, xi=xi, c=c: e.transpose(out=o, in_=xn[xi][:, c * 128:(c + 1) * 128], identity=ident),
                             reads=[B_xn[xi], B_cst], writes=[PSB[b_]])
                for c in range(8):
                    b_ = bk if c < 4 else bk2
                    src = PS[b_][:].bitcast(BF16)[:, (c % 4) * 256:(c % 4) * 256 + 256]
                    P.op('act', lambda e, src=src, c=c, l=l: e.activation(out=mT[:, c, :], in_=src, func=AF.Identity, scale=col('mem%d' % l, c)),
                         reads=[PSB[b_], B_misc], writes=[B_mT])
                for half in range(2):
                    sl, wk, bw = W.acquire(('wkv', l, half * 512))
                    for jj in range(4):
                        j = half * 4 + jj
                        b_ = psrot.next()
                        for c in range(8):
                            P.op('pe', lambda e, b_=b_, c=c, jj=jj, wk=wk: e.matmul(PS[b_][:, 0:256], lhsT=wk[:, c, jj * 128:(jj + 1) * 128],
                                                                                   rhs=mT[:, c, :], start=(c == 0), stop=(c == 7)),
                                 reads=[bw, B_mT], writes=[PSB[b_]])
                        P.op('act', lambda e, b_=b_, j=j: e.copy(out=kT[:, j, :], in_=PS[b_][:, 0:256]), reads=[PSB[b_]], writes=[B_kT])
                    W.release(sl)
                for half in range(2):
                    sl, wv, bw = W.acquire(('wkv', l, 1024 + half * 512))
                    for mt in range(2):
                        b_ = psrot.next()
                        for c in range(8):
                            P.op('pe', lambda e, b_=b_, c=c, mt=mt, wv=wv: e.matmul(PS[b_][:], lhsT=mT[:, c, mt * 128:(mt + 1) * 128],
                                                                                   rhs=wv[:, c, :], start=(c == 0), stop=(c == 7)),
                                 reads=[bw, B_mT], writes=[PSB[b_]])
                        P.op('act', lambda e, b_=b_, mt=mt, half=half: e.copy(out=Vt[:, mt, half * 512:(half + 1) * 512], in_=PS[b_][:]),
                             reads=[PSB[b_]], writes=[B_V])
                    W.release(sl)
                P.dma('sp', d_kvs[l], kv_scr[l], kvt, reads=[B_kT, B_V], writes=[B_kvs[l]])

        def stage_xa(l, s, hook, pre):
            fb = P.fence()
            B_kT = Buf(fb); B_V = B_kT
            qTb = [AR[:, i * 4096:(i + 1) * 4096].rearrange("p (a b) -> p a b", a=8) for i in range(2)]
            oTb = [AR[:, 8192 + i * 4096:8192 + (i + 1) * 4096].rearrange("p (a b) -> p a b", a=8) for i in range(2)]
            kT = AR[:, 16384:18432].rearrange("p (a b) -> p a b", a=8)
            Vt = AR[:, 18432:20480].rearrange("p (a b) -> p a b", a=2)
            PT = [AR[:, 20480 + i * 512:20480 + (i + 1) * 512] for i in range(6)]
            B_qT = [Buf(fb), Buf(fb)]; B_oT = [Buf(fb), Buf(fb)]
            P.dma('sp', d_kvl, AR[:, 16384:20480], kv_scr[l], reads=[B_kvs[l]], writes=[B_kT])
            if not pre:
                norm_to_hT('xa%d' % l)
            slq = [W.acquire(('wq', l, 0)), W.acquire(('wq', l, 512))]
            slo = [W.acquire(('wo', l, 0)), W.acquire(('wo', l, 512))]
            rden = FA[:, 0:512]
            fbx = P.fence()
            B_rden = Buf(fbx)
            B_PT = [Buf(fbx) for _ in range(6)]

            def qproj(tb):
                qT = qTb[tb % 2]
                for j in range(8):
                    _, wq, bw = slq[j // 4]
                    b_ = psrot.next()
                    for c in range(8):
                        P.op('pe', lambda e, b_=b_, c=c, j=j, wq=wq, tb=tb: e.matmul(
                            PS[b_][:], lhsT=wq[:, c, (j % 4) * 128:(j % 4 + 1) * 128], rhs=hT[:, c, tb * 512:(tb + 1) * 512],
                            start=(c == 0), stop=(c == 7)), reads=[bw, HB[tb]], writes=[PSB[b_]])
                    P.op('act', lambda e, b_=b_, j=j, qT=qT: e.activation(out=qT[:, j, :], in_=PS[b_][:], func=AF.Copy, scale=1.0 / 16.0),
                         reads=[PSB[b_]], writes=[B_qT[tb % 2]])

            def s1(tb, hd):
                qT = qTb[tb % 2]
                for mc in range(2):
                    b_ = psrot.next()
                    pi = (hd % 3) * 2 + mc
                    for dd in range(2):
                        j = hd * 2 + dd
                        P.op('pe', lambda e, b_=b_, j=j, mc=mc, dd=dd, qT=qT: e.matmul(
                            PS[b_][:], lhsT=kT[:, j, mc * 128:(mc + 1) * 128], rhs=qT[:, j, :],
                            start=(dd == 0), stop=(dd == 1)), reads=[B_kT, B_qT[tb % 2]], writes=[PSB[b_]])
                    P.op('act', lambda e, b_=b_, pi=pi: e.activation(out=PT[pi], in_=PS[b_][:], func=AF.Exp),
                         reads=[PSB[b_]], writes=[B_PT[pi]])

            def s2(tb, hd):
                oT = oTb[tb % 2]
                pts = [(hd % 3) * 2, (hd % 3) * 2 + 1]
                bden = psrot.next()
                for mc in range(2):
                    P.op('pe', lambda e, bden=bden, mc=mc, pi=pts[mc]: e.matmul(PS[bden][:], lhsT=ones_b, rhs=PT[pi],
                                                                               start=(mc == 0), stop=(mc == 1)),
                         reads=[B_PT[pts[mc]], B_cst], writes=[PSB[bden]])
                P.op('act', lambda e, bden=bden: e.activation(out=rden, in_=PS[bden][:], func=AF.Ln), reads=[PSB[bden]], writes=[B_rden])
                P.op('act', lambda e: e.activation(out=rden, in_=rden, func=AF.Exp, scale=-1.0), reads=[B_rden], writes=[B_rden])
                for dd in range(2):
                    j = hd * 2 + dd
                    b_ = psrot.next()
                    for mc in range(2):
                        P.op('pe', lambda e, b_=b_, j=j, mc=mc, pi=pts[mc]: e.matmul(
                            PS[b_][:], lhsT=Vt[:, mc, j * 128:(j + 1) * 128], rhs=PT[pi], start=(mc == 0), stop=(mc == 1)),
                            reads=[B_V, B_PT[pts[mc]]], writes=[PSB[b_]])
                    P.op('dve', lambda e, b_=b_, j=j, oT=oT: e.tensor_tensor(out=oT[:, j, :], in0=PS[b_][:], in1=rden, op=ALU.mult),
                         reads=[PSB[b_], B_rden], writes=[B_oT[tb % 2]])

            def wo_proj(tb):
                oT = oTb[tb % 2]
                for i in range(4):
                    ti = tb * 4 + i
                    for half in range(2):
                        _, wo, bw = slo[half]
                        b_ = psrot.next()
                        for j in range(8):
                            P.op('pe', lambda e, b_=b_, j=j, i=i, wo=wo, oT=oT: e.matmul(PS[b_][:], lhsT=oT[:, j, i * 128:(i + 1) * 128],
                                                                                        rhs=wo[:, j, :], start=(j == 0), stop=(j == 7)),
                                 reads=[bw, B_oT[tb % 2]], writes=[PSB[b_]])
                        resid_add(ti, half, b_)

            qproj(0)
            for tb in range(4):
                s1(tb, 0)
                for hd in range(4):
                    if hd + 1 < 4:
                        s1(tb, hd + 1)
                    s2(tb, hd)
                    if hd == 1 and tb + 1 < 4:
                        qproj(tb + 1)
                        if tb + 1 == 3:
                            for sl, _, _ in slq:
                                W.release(sl)
                if tb > 0:
                    wo_proj(tb - 1)
                    hook(tb - 1)
            wo_proj(3)
            hook(3)
            for sl, _, _ in slo:
                W.release(sl)


        B_cumd = Buf()
        d_cum = P.dma_sem()
        d_kE = P.dma_sem(); d_kO = P.dma_sem()
        d_qE = [P.dma_sem(), P.dma_sem()]; d_qO = [P.dma_sem(), P.dma_sem()]

        def stage_mix0(s, hook, pre):
            fb = P.fence()
            vgn = AR[:, 0:8192].rearrange("p (a b) -> p a b", a=16)
            u_sb = AR[:, 8192:16384].rearrange("p (a b) -> p a b", a=16)
            kE = AR[:, 16384:18432]; kO = AR[:, 18432:20480]
            qE = [AR[:, 20480 + i * 512:20480 + (i + 1) * 512] for i in range(2)]
            qO = [AR[:, 21504 + i * 512:21504 + (i + 1) * 512] for i in range(2)]
            PTa = [AR[:, 22528 + i * 512:22528 + (i + 1) * 512] for i in range(2)]
            aT = [PTa[i].rearrange("p (a b) -> p a b", a=4) for i in range(2)]
            vgf = [FA[:, 0:512], FA[:, 512:1024]]
            tmpv = FA[:, 1024:1536]
            rden = FA[:, 1536:2048]
            small2 = FA[:, 2048:2560]
            sp6 = AR[:, 16384:17152].rearrange("p (a b) -> p a b", a=6)
            st6 = AR[:, 17152:17920]
            B_vgn = Buf(fb); B_u = Buf(fb); B_vgf = [Buf(fb), Buf(fb)]; B_tmpv = Buf(fb); B_aT = [Buf(fb), Buf(fb)]
            B_f = Buf(fb)
            slu, wzu, bzu = W.acquire(('in_e', 1544))
            slv, wzv, bzv = W.acquire(('in_e', 2056))
            def zproj(ti):
                ba, bb = psrot.next(), psrot.next()
                for c in range(8):
                    P.op('pe', lambda e, ba=ba, c=c, ti=ti: e.matmul(PS[ba][:], lhsT=hT[:, c, ti * 128:(ti + 1) * 128], rhs=wzu[:, c, :],
                                                                   start=(c == 0), stop=(c == 7)), reads=[bzu, HB[ti // 4]], writes=[PSB[ba]])
                for c in range(8):
                    P.op('pe', lambda e, bb=bb, c=c, ti=ti: e.matmul(PS[bb][:], lhsT=hT[:, c, ti * 128:(ti + 1) * 128], rhs=wzv[:, c, :],
                                                                   start=(c == 0), stop=(c == 7)), reads=[bzv, HB[ti // 4]], writes=[PSB[bb]])
                k = ti % 2
                so = 32 + k * 16
                P.op('act', lambda e, ba=ba, ti=ti: e.activation(out=u_sb[:, ti, :], in_=PS[ba][:], func=AF.Gelu_apprx_tanh),
                     reads=[PSB[ba]], writes=[B_u])
                P.op('act', lambda e, bb=bb, k=k: e.activation(out=vgf[k], in_=PS[bb][:], func=AF.Gelu_apprx_tanh),
                     reads=[PSB[bb]], writes=[B_vgf[k]])
                P.op('dve', lambda e, k=k, so=so: e.bn_stats(out=small[:, so:so + 6], in_=vgf[k]), reads=[B_vgf[k]], writes=[B_small])
                P.op('dve', lambda e, so=so: e.bn_aggr(out=small[:, so + 6:so + 8], in_=small[:, so:so + 6]), reads=[B_small], writes=[B_small])
                P.op('act', lambda e, so=so: e.activation(out=small[:, so + 8:so + 9], in_=small[:, so + 7:so + 8], func=AF.Sqrt, bias=col_eps()),
                     reads=[B_small, B_cst], writes=[B_small])
                P.op('dve', lambda e, so=so: e.reciprocal(out=small[:, so + 9:so + 10], in_=small[:, so + 8:so + 9]), reads=[B_small], writes=[B_small])
                P.op('dve', lambda e, k=k, so=so: e.scalar_tensor_tensor(out=tmpv, in0=vgf[k], scalar=small[:, so + 6:so + 7], in1=rows[:, 1024:1536],
                                                                        op0=ALU.subtract, op1=ALU.mult),
                     reads=[B_vgf[k], B_small, B_misc], writes=[B_tmpv])
                P.op('dve', lambda e, ti=ti, so=so: e.scalar_tensor_tensor(out=vgn[:, ti, :], in0=tmpv, scalar=small[:, so + 9:so + 10], in1=rows[:, 1536:2048],
                                                                          op0=ALU.mult, op1=ALU.add),
                     reads=[B_tmpv, B_small, B_misc], writes=[B_vgn])

            if pre:
                for ti in range(NT):
                    zproj(ti)
            else:
                norm_group_stats(0)
                for g in range(4):
                    if g + 1 < 4:
                        norm_group_stats(g + 1)
                    norm_group_apply('mix_e', g)
                    if g >= 1:
                        for ti in range((g - 1) * 4, g * 4):
                            zproj(ti)
                for ti in range(12, 16):
                    zproj(ti)
            W.release(slu)
            W.release(slv)
            bf_ = psrot.next()
            for ti in range(NT):
                for c in range(8):
                    P.op('pe', lambda e, c=c, ti=ti: e.matmul(PS[bf_][:, ti * 8:(ti + 1) * 8], lhsT=hT[:, c, ti * 128:(ti + 1) * 128], rhs=wf[:, c, :],
                                                            start=(c == 0), stop=(c == 7)), reads=[B_misc, HB[ti // 4]], writes=[PSB[bf_]])
            sp32 = small2[:, 0:128]
            P.op('dve', lambda e: e.tensor_tensor(out=sp32, in0=PS[bf_][:, 0:128], in1=rows[:, 2048:2176], op=ALU.add),
                 reads=[PSB[bf_], B_misc], writes=[B_f])
            P.op('act', lambda e: e.activation(out=sp32, in_=sp32, func=AF.Exp, scale=-1.0), reads=[B_f], writes=[B_f])
            P.op('act', lambda e: e.activation(out=sp32, in_=sp32, func=AF.Ln, bias=1.0), reads=[B_f], writes=[B_f])
            slw, woG, bwoG = W.acquire(('out_e', 4))
            for half in range(2):
                for g in range(8):
                    b_ = psrot.next()
                    P.op('pe', lambda e, b_=b_, g=g, half=half: e.matmul(PS[b_][:], lhsT=wmT[:, g, :],
                                                                       rhs=vgn[:, half * 8:(half + 1) * 8, g * 64:(g + 1) * 64],
                                                                       start=True, stop=True), reads=[B_vgn, B_cst], writes=[PSB[b_]])
                    P.op('dve', lambda e, b_=b_, g=g, half=half: e.scalar_tensor_tensor(
                        out=u_sb[:, half * 8:(half + 1) * 8, g * 64:(g + 1) * 64],
                        in0=PS[b_][:].rearrange("p (a b) -> p a b", a=8), scalar=col('gbs', g),
                        in1=u_sb[:, half * 8:(half + 1) * 8, g * 64:(g + 1) * 64], op0=ALU.add, op1=ALU.mult),
                        reads=[PSB[b_], B_misc], writes=[B_u])
            bt_ = psrot.next()
            P.op('pe', lambda e: e.matmul(PS[bt_][:, 0:128], lhsT=sp32, rhs=c32[:, 1, :], start=True, stop=True),
                 reads=[B_f, B_misc], writes=[PSB[bt_]])
            totrep = small2[:, 128:256]
            P.op('dve', lambda e: e.tensor_copy(out=totrep, in_=PS[bt_][:, 0:128]), reads=[PSB[bt_]], writes=[B_f])
            bc_ = psrot.next()
            P.op('pe', lambda e: e.matmul(PS[bc_][:, 0:128], lhsT=c32[:, 0, :], rhs=sp32, start=True, stop=False),
                 reads=[B_f, B_misc], writes=[PSB[bc_]])
            P.op('pe', lambda e: e.matmul(PS[bc_][:, 0:128], lhsT=totrep, rhs=c32[:, 2, :], start=False, stop=True),
                 reads=[B_f, B_misc], writes=[PSB[bc_]])
            ncum = small2[:, 256:384]
            r1 = small2[:, 384:512]
            P.op('dve', lambda e: e.tensor_copy(out=ncum, in_=PS[bc_][:, 0:128]), reads=[PSB[bc_]], writes=[B_f])
            P.op('pool', lambda e: e.tensor_copy(out=sp6[:, 3, :], in_=ncum), reads=[B_f], writes=[B_f])
            P.op('pool', lambda e: e.tensor_tensor(out=r1, in0=ncum, in1=sp6[:, 3, :], op=ALU.subtract), reads=[B_f], writes=[B_f])
            P.op('pool', lambda e: e.tensor_copy(out=sp6[:, 4, :], in_=r1), reads=[B_f], writes=[B_f])
            P.op('pool', lambda e: e.tensor_tensor(out=r1, in0=r1, in1=sp6[:, 4, :], op=ALU.subtract), reads=[B_f], writes=[B_f])
            P.op('pool', lambda e: e.tensor_copy(out=sp6[:, 5, :], in_=r1), reads=[B_f], writes=[B_f])
            P.op('pool', lambda e: e.tensor_scalar(out=sp6[:, 0:3, :], in0=sp6[:, 3:6, :], scalar1=-1.0, scalar2=0.0, op0=ALU.mult, op1=ALU.add),
                 reads=[B_f], writes=[B_f])
            for ti in range(NT):
                k = ti % 2
                b_ = psrot.next()
                for cg in range(4):
                    o = PS[b_][:].bitcast(BF16)[:, cg * 128:(cg + 1) * 128]
                    P.op('pe', lambda e, o=o, ti=ti, cg=cg: e.transpose(out=o, in_=u_sb[:, ti, cg * 128:(cg + 1) * 128], identity=ident),
                         reads=[B_u, B_cst], writes=[PSB[b_]])
                P.op('dve', lambda e, b_=b_, k=k: e.tensor_copy(out=PTa[k], in_=PS[b_][:].bitcast(BF16)[:, 0:512]), reads=[PSB[b_]], writes=[B_aT[k]])
                for half in range(2):
                    b2 = psrot.next()
                    for cg in range(4):
                        P.op('pe', lambda e, b2=b2, cg=cg, k=k, half=half: e.matmul(PS[b2][:], lhsT=aT[k][:, cg, :],
                                                                                  rhs=woG[:, cg, half * 512:(half + 1) * 512],
                                                                                  start=(cg == 0), stop=(cg == 3)),
                             reads=[B_aT[k], bwoG], writes=[PSB[b2]])
                    resid_add(ti, half, b2)
            W.release(slw)
            bx_ = psrot.next()
            for k6 in range(6):
                o = PS[bx_][:].bitcast(BF16)[:, k6 * 128:(k6 + 1) * 128]
                P.op('pe', lambda e, o=o, k6=k6: e.transpose(out=o, in_=sp6[:, k6, :], identity=ident), reads=[B_f, B_cst], writes=[PSB[bx_]])
            P.op('act', lambda e: e.copy(out=st6, in_=PS[bx_][:].bitcast(BF16)[:, 0:768]), reads=[PSB[bx_]], writes=[B_f])
            P.dma('sp', d_cum, cumd, st6, reads=[B_f], writes=[B_cumd])
            cum4 = cumd.rearrange("(j h) (k t) -> h k j t", h=8, k=6)
            fb2 = P.fence()
            vt = AR[:, 0:8192].rearrange("p (a b) -> p a b", a=16)
            foxT = AR[:, 8192:16384].rearrange("p (a b) -> p a b", a=4)
            B_v = Buf(fb2); B_fox = Buf(fb2); B_PT = [Buf(fb2), Buf(fb2)]; B_rden = Buf(fb2)
            B_kE = Buf(fb2); B_kO = Buf(fb2); B_qE = [Buf(fb2), Buf(fb2)]; B_qO = [Buf(fb2), Buf(fb2)]
            slv2, wv, bwv = W.acquire(('in_e', 1024))
            for ti in range(NT):
                b_ = psrot.next()
                for c in range(8):
                    P.op('pe', lambda e, b_=b_, c=c, ti=ti: e.matmul(PS[b_][:], lhsT=hT[:, c, ti * 128:(ti + 1) * 128], rhs=wv[:, c, :],
                                                                   start=(c == 0), stop=(c == 7)), reads=[bwv, HB[ti // 4]], writes=[PSB[b_]])
                P.op('dve', lambda e, b_=b_, ti=ti: e.tensor_copy(out=vt[:, ti, :], in_=PS[b_][:]), reads=[PSB[b_]], writes=[B_v])
            W.release(slv2)
            for tl, bb_ in [(kE, B_kE), (qE[0], B_qE[0]), (qE[1], B_qE[1])]:
                P.op('pool', lambda e, tl=tl: e.memset(tl[64:70, :], 1.0), writes=[bb_])
            for tl, bb_ in [(kO, B_kO), (qO[0], B_qO[0]), (qO[1], B_qO[1])]:
                P.op('pool', lambda e, tl=tl: e.memset(tl[0:64, :], 0.0), writes=[bb_])
                P.op('pool', lambda e, tl=tl: e.memset(tl[0:6, :], 1.0), writes=[bb_])
            slq, wq, bwq = W.acquire(('in_e', 0))
            slk, wk, bwk = W.acquire(('in_e', 512))
            slf, woF, bwoF = W.acquire(('out_e', 0))
            strot = Rot([4, 5, 6, 7])
            PTl = [PTa[0], PTa[1]] + [FA[:, i * 256:(i + 1) * 256].bitcast(BF16) for i in range(4)]
            B_PTl = [Buf(fb2) for _ in range(6)]
            LA = 3
            unit_ctr = [0]
            for hp in range(4):
                hE, hO = 2 * hp, 2 * hp + 1
                for tb in range(4):
                    b_ = strot.next()
                    for c in range(8):
                        P.op('pe', lambda e, b_=b_, c=c, tb=tb, hp=hp: e.matmul(PS[b_][:], lhsT=wk[:, c, hp * 128:(hp + 1) * 128],
                                                                              rhs=hT[:, c, tb * 512:(tb + 1) * 512], start=(c == 0), stop=(c == 7)),
                             reads=[bwk, HB[tb]], writes=[PSB[b_]])
                    P.op('act', lambda e, b_=b_, tb=tb: e.copy(out=kE[0:64, tb * 512:(tb + 1) * 512], in_=PS[b_][0:64, :]),
                         reads=[PSB[b_]], writes=[B_kE])
                    P.op('act', lambda e, b_=b_, tb=tb: e.copy(out=kO[64:128, tb * 512:(tb + 1) * 512], in_=PS[b_][64:128, :]),
                         reads=[PSB[b_]], writes=[B_kO])
                P.dma('sp', d_kE, kE[67:70, :].rearrange("p (j t) -> p j t", j=16), cum4[hE, 3:6], reads=[B_cumd], writes=[B_kE])
                P.dma('sp', d_kO, kO[3:6, :].rearrange("p (j t) -> p j t", j=16), cum4[hO, 3:6], reads=[B_cumd], writes=[B_kO])

                def qproj(QB, hp=hp, hE=hE, hO=hO):
                    qi = QB % 2
                    b_ = strot.next()
                    for c in range(8):
                        P.op('pe', lambda e, b_=b_, c=c, QB=QB, hp=hp: e.matmul(PS[b_][:], lhsT=wq[:, c, hp * 128:(hp + 1) * 128],
                                                                              rhs=hT[:, c, QB * 512:(QB + 1) * 512], start=(c == 0), stop=(c == 7)),
                             reads=[bwq, HB[QB]], writes=[PSB[b_]])
                    P.op('act', lambda e, b_=b_, qi=qi: e.activation(out=qE[qi][0:64, :], in_=PS[b_][0:64, :], func=AF.Copy, scale=0.125),
                         reads=[PSB[b_]], writes=[B_qE[qi]])
                    P.op('act', lambda e, b_=b_, qi=qi: e.activation(out=qO[qi][64:128, :], in_=PS[b_][64:128, :], func=AF.Copy, scale=0.125),
                         reads=[PSB[b_]], writes=[B_qO[qi]])
                    P.dma('sp', d_qE[qi], qE[qi][64:67, :].rearrange("p (j t) -> p j t", j=4), cum4[hE, 0:3, QB * 4:(QB + 1) * 4],
                          reads=[B_cumd], writes=[B_qE[qi]])
                    P.dma('sp', d_qO[qi], qO[qi][0:3, :].rearrange("p (j t) -> p j t", j=4), cum4[hO, 0:3, QB * 4:(QB + 1) * 4],
                          reads=[B_cumd], writes=[B_qO[qi]])

                tasks = []
                for QB in range(4):
                    for par in range(2):
                        bo_, bd_ = (0, 1) if unit_ctr[0] % 2 == 0 else (2, 3)
                        unit_ctr[0] += 1
                        for kc in range(4 * QB + 4):
                            tasks.append((QB, par, kc, bo_, bd_))

                def operands(QB, par):
                    qi = QB % 2
                    if par == 0:
                        return kE[0:70, :], qE[qi][0:70, :], B_kE, B_qE[qi], 0, 64, hE * 64
                    return kO, qO[qi], B_kO, B_qO[qi], 64, 128, (hO - 1) * 64

                def st1(i):
                    QB, par, kc, bo_, bd_ = tasks[i]
                    if par == 0 and kc == 0 and QB + 1 < 4:
                        qproj(QB + 1)
                    kt, qt, bk_, bq_, r0, r1_, vlo = operands(QB, par)
                    r = kc - 4 * QB
                    n0 = max(r, 0) * 128
                    bs_ = strot.next()
                    P.op('pe', lambda e, bs_=bs_, kc=kc, n0=n0, r=r, kt=kt, qt=qt: e.matmul(
                        PS[bs_][:, n0:512], lhsT=kt[:, kc * 128:(kc + 1) * 128], rhs=qt[:, n0:512], start=True, stop=(r < 0)),
                        reads=[bk_, bq_], writes=[PSB[bs_]])
                    if r >= 0:
                        P.op('pe', lambda e, bs_=bs_, n0=n0: e.matmul(PS[bs_][:, n0:n0 + 128], lhsT=ident, rhs=maskadd, start=False, stop=True),
                             reads=[B_cst], writes=[PSB[bs_]])
                    pi = i % 6
                    P.op('act', lambda e, bs_=bs_, pi=pi, n0=n0: e.activation(out=PTl[pi][:, n0:512], in_=PS[bs_][:, n0:512], func=AF.Exp),
                         reads=[PSB[bs_]], writes=[B_PTl[pi]])

                def st2(i):
                    QB, par, kc, bo_, bd_ = tasks[i]
                    kt, qt, bk_, bq_, r0, r1_, vlo = operands(QB, par)
                    r = kc - 4 * QB
                    n0 = max(r, 0) * 128
                    nkc = 4 * QB + 4
                    pi = i % 6
                    P.op('pe', lambda e, kc=kc, pi=pi, n0=n0, vlo=vlo, nkc=nkc, bo_=bo_: e.matmul(
                        PS[bo_][:, n0:512], lhsT=vt[:, kc, vlo:vlo + 128], rhs=PTl[pi][:, n0:512], start=(kc == 0), stop=(kc == nkc - 1)),
                        reads=[B_v, B_PTl[pi]], writes=[PSB[bo_]])
                    P.op('pe', lambda e, kc=kc, pi=pi, n0=n0, nkc=nkc, bd_=bd_: e.matmul(
                        PS[bd_][:, n0:512], lhsT=ones_b, rhs=PTl[pi][:, n0:512], start=(kc == 0), stop=(kc == nkc - 1)),
                        reads=[B_cst, B_PTl[pi]], writes=[PSB[bd_]])
                    if kc == nkc - 1:
                        P.op('dve', lambda e, bd_=bd_, r0=r0, r1_=r1_: e.reciprocal(out=rden[r0:r1_, :], in_=PS[bd_][r0:r1_, :]),
                             reads=[PSB[bd_]], writes=[B_rden])
                        P.op('dve', lambda e, bo_=bo_, r0=r0, r1_=r1_, hp=hp, QB=QB: e.tensor_tensor(
                            out=foxT[r0:r1_, hp, QB * 512:(QB + 1) * 512], in0=PS[bo_][r0:r1_, :], in1=rden[r0:r1_, :], op=ALU.mult),
                            reads=[PSB[bo_], B_rden], writes=[B_fox])

                qproj(0)
                nt_ = len(tasks)
                for i in range(nt_ + LA):
                    if i < nt_:
                        st1(i)
                    if i >= LA:
                        st2(i - LA)
            W.release(slq)
            W.release(slk)
            for ti in range(NT):
                for half in range(2):
                    b_ = psrot.next()
                    for hp in range(4):
                        P.op('pe', lambda e, b_=b_, hp=hp, ti=ti, half=half: e.matmul(PS[b_][:], lhsT=foxT[:, hp, ti * 128:(ti + 1) * 128],
                                                                                    rhs=woF[:, hp, half * 512:(half + 1) * 512],
                                                                                    start=(hp == 0), stop=(hp == 3)),
                             reads=[B_fox, bwoF], writes=[PSB[b_]])
                    resid_add(ti, half, b_)
                if ti % 4 == 3:
                    hook(ti // 4)
            W.release(slf)

        def stage_mix1(hook, pre):
            if not pre:
                norm_to_hT('mix_o')
            fb = P.fence()
            yT = AR[:, 0:16624].rearrange("p (a b) -> p a b", a=8)
            sigt = [AR[:, 20592 + i * 512:20592 + (i + 1) * 512] for i in range(2)]
            B_y = Buf(fb); B_sig = [Buf(fb), Buf(fb)]
            P.op('pool', lambda e: e.memset(yT[:, :, 0:30], 0.0), writes=[B_y])
            sgr = Rot([0, 1])
            for hf in range(2):
                sla, wa, bwa = W.acquire(('cin', hf * 512))
                slg, wg, bwg = W.acquire(('cin', 1024 + hf * 512))
                for c4 in range(4):
                    cc = hf * 4 + c4
                    for tb in range(4):
                        ba, bb = psrot.next(), psrot.next()
                        for c in range(8):
                            P.op('pe', lambda e, ba=ba, c=c, c4=c4, tb=tb, wa=wa: e.matmul(
                                PS[ba][:], lhsT=wa[:, c, c4 * 128:(c4 + 1) * 128], rhs=hT[:, c, tb * 512:(tb + 1) * 512],
                                start=(c == 0), stop=(c == 7)), reads=[bwa, HB[tb]], writes=[PSB[ba]])
                        for c in range(8):
                            P.op('pe', lambda e, bb=bb, c=c, c4=c4, tb=tb, wg=wg: e.matmul(
                                PS[bb][:], lhsT=wg[:, c, c4 * 128:(c4 + 1) * 128], rhs=hT[:, c, tb * 512:(tb + 1) * 512],
                                start=(c == 0), stop=(c == 7)), reads=[bwg, HB[tb]], writes=[PSB[bb]])
                        si = sgr.next()
                        P.op('act', lambda e, bb=bb, si=si, cc=cc: e.activation(out=sigt[si], in_=PS[bb][:], func=AF.Sigmoid,
                                                                              bias=col('cbin', 8 + cc)),
                             reads=[PSB[bb], B_misc], writes=[B_sig[si]])
                        P.op('dve', lambda e, ba=ba, si=si, cc=cc, tb=tb: e.scalar_tensor_tensor(
                            out=yT[:, cc, 30 + tb * 512:30 + (tb + 1) * 512], in0=PS[ba][:], scalar=col('cbin', cc),
                            in1=sigt[si], op0=ALU.add, op1=ALU.mult), reads=[PSB[ba], B_sig[si], B_misc], writes=[B_y])
                W.release(sla)
                W.release(slg)
            c_sb = hT
            for cc in range(8):
                sld, diag, bdg = W.acquire(('diag', cc))
                for tb in range(4):
                    b_ = psrot.next()
                    for j in range(31):
                        P.op('pe', lambda e, b_=b_, j=j, cc=cc, tb=tb, diag=diag: e.matmul(
                            PS[b_][:], lhsT=diag[:, j, :], rhs=yT[:, cc, tb * 512 + j:tb * 512 + j + 512],
                            start=(j == 0), stop=(j == 30)), reads=[bdg, B_y], writes=[PSB[b_]])
                    P.op('act', lambda e, b_=b_, cc=cc, tb=tb: e.activation(out=c_sb[:, cc, tb * 512:(tb + 1) * 512], in_=PS[b_][:],
                                                                          func=AF.Identity, bias=col('dwb', cc)),
                         reads=[PSB[b_], B_misc], writes=[HB[tb]])
                W.release(sld)
            fb3 = P.fence()
            sqb = [AR[:, i * 4096:(i + 1) * 4096].rearrange("p (a b) -> p a b", a=8) for i in range(2)]
            nTb = [AR[:, 8192 + i * 4096:8192 + (i + 1) * 4096].rearrange("p (a b) -> p a b", a=8) for i in range(2)]
            tmpn = [AR[:, 16384 + i * 1024:16384 + (i + 1) * 1024].bitcast(F32) for i in range(2)]
            arf = [AR[:, 18432 + i * 1024:18432 + (i + 1) * 1024].bitcast(F32) for i in range(4)]
            B_sq = [Buf(fb3), Buf(fb3)]; B_nT = [Buf(fb3), Buf(fb3)]; B_fa = [Buf(fb3) for _ in range(4)]
            m_tb = [FA[:, i * 512:(i + 1) * 512] for i in range(4)]
            v_tb = [FA[:, 2048:2560], arf[0], arf[1], arf[2]]
            t1 = arf[3]
            B_t1 = Buf(fb3)
            B_tmp = [Buf(fb3), Buf(fb3)]
            slc = [W.acquire(('cout', 0)), W.acquire(('cout', 512))]
            tr = Rot([0, 1])

            def ln_stats(tb):
                k2 = tb % 2
                sq = sqb[k2]; m_t = m_tb[tb]; v_t = v_tb[tb]
                for cc in range(8):
                    P.op('act', lambda e, cc=cc, tb=tb, sq=sq: e.activation(out=sq[:, cc, :], in_=c_sb[:, cc, tb * 512:(tb + 1) * 512], func=AF.Square),
                         reads=[HB[tb]], writes=[B_sq[k2]])
                b1, b2 = psrot.next(), psrot.next()
                for cc in range(8):
                    P.op('pe', lambda e, b1=b1, cc=cc, tb=tb: e.matmul(PS[b1][:], lhsT=ones_b, rhs=c_sb[:, cc, tb * 512:(tb + 1) * 512],
                                                                     start=(cc == 0), stop=(cc == 7)), reads=[HB[tb], B_cst], writes=[PSB[b1]])
                for cc in range(8):
                    P.op('pe', lambda e, b2=b2, cc=cc, sq=sq: e.matmul(PS[b2][:], lhsT=ones_b, rhs=sq[:, cc, :],
                                                                     start=(cc == 0), stop=(cc == 7)), reads=[B_sq[k2], B_cst], writes=[PSB[b2]])
                P.op('act', lambda e, b1=b1, m_t=m_t: e.activation(out=m_t, in_=PS[b1][:], func=AF.Copy, scale=1.0 / D),
                     reads=[PSB[b1]], writes=[B_fa[tb]])
                P.op('dve', lambda e, m_t=m_t: e.tensor_tensor(out=t1, in0=m_t, in1=m_t, op=ALU.mult), reads=[B_fa[tb]], writes=[B_t1])
                P.op('dve', lambda e, b2=b2, v_t=v_t: e.scalar_tensor_tensor(out=v_t, in0=PS[b2][:], scalar=1.0 / D, in1=t1,
                                                                           op0=ALU.mult, op1=ALU.subtract),
                     reads=[PSB[b2], B_t1], writes=[B_fa[tb]])
                P.op('act', lambda e, v_t=v_t: e.activation(out=v_t, in_=v_t, func=AF.Ln, bias=col_eps()), reads=[B_fa[tb], B_cst], writes=[B_fa[tb]])
                P.op('act', lambda e, v_t=v_t: e.activation(out=v_t, in_=v_t, func=AF.Exp, scale=-0.5), reads=[B_fa[tb]], writes=[B_fa[tb]])

            def ln_apply(tb):
                k2 = tb % 2
                m_t = m_tb[tb]; v_t = v_tb[tb]; nTt = nTb[k2]
                for cc in range(8):
                    k = tr.next()
                    en = 'pool' if cc % 2 == 0 else 'dve'
                    P.op(en, lambda e, cc=cc, tb=tb, k=k, m_t=m_t: e.tensor_tensor(out=tmpn[k], in0=c_sb[:, cc, tb * 512:(tb + 1) * 512],
                                                                                 in1=m_t, op=ALU.subtract),
                         reads=[HB[tb], B_fa[tb]], writes=[B_tmp[k]])
                    P.op(en, lambda e, k=k, v_t=v_t: e.tensor_tensor(out=tmpn[k], in0=tmpn[k], in1=v_t, op=ALU.mult),
                         reads=[B_fa[tb]], writes=[B_tmp[k]])
                    P.op('act', lambda e, cc=cc, k=k, nTt=nTt: e.activation(out=nTt[:, cc, :], in_=tmpn[k], func=AF.Silu,
                                                                           scale=col('clng', cc), bias=col('clnb', cc)),
                         reads=[B_tmp[k], B_misc], writes=[B_nT[k2]])

            def out_proj(tb):
                k2 = tb % 2
                nTt = nTb[k2]
                for i in range(4):
                    ti = tb * 4 + i
                    for half in range(2):
                        _, wo_, bw = slc[half]
                        b_ = psrot.next()
                        P.op('pe', lambda e, b_=b_, half=half: e.matmul(PS[b_][:], lhsT=cst[0:1, 3, :], rhs=bout_b[0:1, half * 512:(half + 1) * 512],
                                                                      start=True, stop=False), reads=[B_cst, B_misc], writes=[PSB[b_]])
                        for cc in range(8):
                            P.op('pe', lambda e, b_=b_, cc=cc, i=i, wo_=wo_, nTt=nTt: e.matmul(PS[b_][:], lhsT=nTt[:, cc, i * 128:(i + 1) * 128],
                                                                                              rhs=wo_[:, cc, :], start=False, stop=(cc == 7)),
                                 reads=[bw, B_nT[k2]], writes=[PSB[b_]])
                        resid_add(ti, half, b_)

            for tb in range(4):
                ln_stats(tb)
            fb5 = P.fence()
            tmpn = [AR[:, i * 1024:(i + 1) * 1024].bitcast(F32) for i in range(4)]
            B_tmp = [Buf(fb5) for _ in range(4)]
            tr = Rot(range(4))
            for tb in range(4):
                ln_apply(tb)
                if tb > 0:
                    out_proj(tb - 1)
                    hook(tb - 1)
            out_proj(3)
            hook(3)
            for sl, _, _ in slc:
                W.release(sl)

        d_out = [P.dma_sem() for _ in range(NT)]
        B_outd = Buf()

        stg = [AR[:, 9216 + k * 2048:9216 + (k + 1) * 2048].bitcast(F32) for k in range(6)]
        B_stg = [None] * 6

        use_stg = stages[-1].startswith('ffn')

        def out_group(s, tb):
            if not use_stg:
                if final_norm:
                    norm_stats(lambda i, tb=tb: x_sb[:, tb * 4 + i, :], 4, XB[tb * 4:tb * 4 + 4], tb * 4)
                for i in range(4):
                    ti = tb * 4 + i
                    if final_norm:
                        P.op('dve', lambda e, ti=ti: e.scalar_tensor_tensor(out=x_sb[:, ti, :], in0=x_sb[:, ti, :], scalar=rstd[:, ti:ti + 1],
                                                                           in1=rows[:, 0:1024], op0=ALU.mult, op1=ALU.mult),
                             reads=[B_small, B_misc], writes=[XB[ti]])
                    P.dma('sp', d_out[ti], out_d[s, ti * 128:(ti + 1) * 128, :], x_sb[:, ti, :], reads=[XB[ti]], writes=[B_outd])
                    if s + 1 < NSEQ:
                        P.dma('sp', d_x[ti], x_sb[:, ti, :], x_d[s + 1, ti * 128:(ti + 1) * 128, :], writes=[XB[ti]])
                return
            if tb == 0:
                fbo = P.fence()
                for k in range(6):
                    B_stg[k] = Buf(fbo)
            if final_norm:
                norm_stats(lambda i, tb=tb: x_sb[:, tb * 4 + i, :], 4, XB[tb * 4:tb * 4 + 4], tb * 4)
            for i in range(4):
                ti = tb * 4 + i
                k = ti % 6
                if final_norm:
                    P.op('dve', lambda e, ti=ti, k=k: e.scalar_tensor_tensor(out=stg[k], in0=x_sb[:, ti, :], scalar=rstd[:, ti:ti + 1],
                                                                            in1=rows[:, 0:1024], op0=ALU.mult, op1=ALU.mult),
                         reads=[XB[ti], B_small, B_misc], writes=[B_stg[k]])
                else:
                    P.op('dve', lambda e, ti=ti, k=k: e.tensor_copy(out=stg[k], in_=x_sb[:, ti, :]), reads=[XB[ti]], writes=[B_stg[k]])
                if s + 1 < NSEQ:
                    P.dma('sp', d_x[ti], x_sb[:, ti, :], x_d[s + 1, ti * 128:(ti + 1) * 128, :], writes=[XB[ti]])
                P.dma('sp', d_out[ti], out_d[s, ti * 128:(ti + 1) * 128, :], stg[k], reads=[B_stg[k]], writes=[B_outd])

        def load_x(s):
            for ti in range(NT):
                P.dma('sp', d_x[ti], x_sb[:, ti, :], x_d[s, ti * 128:(ti + 1) * 128, :], writes=[XB[ti]])

        cast_order = []
        for st in stages:
            cast_order += {'mix0': ['w_in_e', 'w_out_e'], 'xa0': ['wkv0', 'wq0', 'wo0'], 'ffn0': ['wgu0', 'wdn0'],
                           'mix1': ['conv_w_in', 'conv_w_out'], 'xa1': ['wkv1', 'wq1', 'wo1'], 'ffn1': ['wgu1', 'wdn1']}[st]
        do_casts(cast_order)
        if 'mix1' in stages:
            B_db = Buf()
            dstage = AR[:, 0:3968].rearrange("p (a b) -> p a b", a=31)
            for cc in range(8):
                for j in range(31):
                    P.op('pool', lambda e, j=j, cc=cc: e.tensor_scalar(out=dstage[:, j, :], in0=ident, scalar1=col('dww', cc * 31 + j),
                                                                     scalar2=0.0, op0=ALU.mult, op1=ALU.add),
                         reads=[B_cst, B_misc], writes=[B_db])
                kd = P.dma_sem()
                WB['diag%d' % cc] = Buf()
                P.dma('sp', kd, s_diag[cc], AR[:, 0:3968], reads=[B_db], writes=[WB['diag%d' % cc]])
        load_x(0)
        W.start()
        kv_prep(0)
        NORM_NAME = {'mix0': 'mix_e', 'xa0': 'xa0', 'ffn0': 'ffn0', 'mix1': 'mix_o', 'xa1': 'xa1', 'ffn1': 'ffn1'}
        for s in range(NSEQ):
            for si, st in enumerate(stages):
                if si + 1 < len(stages):
                    hook = (lambda tb, g=NORM_NAME[stages[si + 1]]: norm_hook(g, tb))
                else:
                    hook = (lambda tb, s=s: out_group(s, tb))
                pre = si > 0
                P.tag = '%d:%s' % (s, st)
                if st == 'ffn0':
                    if st == stages[-1] and SPLIT_LAST:
                        stage_ffn(0, hook, pre, s, (0, 1))
                        stage_ffn(0, hook, True, s, (2, 3))
                    else:
                        stage_ffn(0, hook, pre, s)
                elif st == 'ffn1':
                    if st == stages[-1] and SPLIT_LAST:
                        stage_ffn(1, hook, pre, s, (0, 1))
                        stage_ffn(1, hook, True, s, (2, 3))
                    else:
                        stage_ffn(1, hook, pre, s)
                elif st == 'xa0':
                    stage_xa(0, s, hook, pre)
                elif st == 'xa1':
                    stage_xa(1, s, hook, pre)
                elif st == 'mix0':
                    stage_mix0(s, hook, pre)
                elif st == 'mix1':
                    stage_mix1(hook, pre)
            if s + 1 < NSEQ:
                P.tag = '%d:kv' % (s + 1)
                kv_prep(s + 1)
        P.final_wait('sp', [B_outd])
        P.emit()
    return nc


def host_prep(inp):
    f = lambda a: np.ascontiguousarray(np.asarray(a, dtype=np.float32))
    cols = np.zeros((128, NCOL), np.float32)

    def put(name, vec):
        c0, n = COLS[name]
        cols[:, c0:c0 + n] = f(vec).reshape(n, 128).T

    put('mix_e', inp['mix_norm_e'][0]); put('xa0', inp['xa_norm'][0]); put('ffn0', inp['ffn_norm'][0])
    put('mix_o', inp['mix_norm_o'][0]); put('xa1', inp['xa_norm'][1]); put('ffn1', inp['ffn_norm'][1])
    put('mem0', inp['mem_norm'][0]); put('mem1', inp['mem_norm'][1])
    put('cbin', inp['conv_b_in'][0]); put('dwb', inp['conv_dw_b'][0])
    put('clng', inp['conv_ln_g'][0]); put('clnb', inp['conv_ln_b'][0])
    c0, n = COLS['dww']
    cols[:, c0:c0 + n] = f(inp['conv_dw_w'][0]).reshape(31, 8, 128).transpose(2, 1, 0).reshape(128, 248)
    c0, n = COLS['gbs']
    cols[:, c0:c0 + n] = f(inp['gmlp_b_s'][0]).T
    c0, n = COLS['fbias']
    cols[0:8, c0] = f(inp['fox_f_bias'][0])
    rows = np.zeros((128, 2176), np.float32)
    rows[:, 0:1024] = f(inp['final_norm'])[None, :]
    rows[:, 1024:1536] = f(inp['gmlp_ln_g'][0])[None, :]
    rows[:, 1536:2048] = f(inp['gmlp_ln_b'][0])[None, :]
    rows[:, 2048:2176] = np.tile(f(inp['fox_f_bias'][0]), 16)[None, :]
    wsT = np.ascontiguousarray(f(inp['gmlp_w_s'][0]).transpose(2, 0, 1))
    cst = np.zeros((128, 4, 128), np.float32)
    idx = np.arange(128)
    cst[:, 0, :] = np.eye(128, dtype=np.float32)
    cst[:, 1, :] = np.where(idx[None, :] >= idx[:, None], 0.0, -30000.0)
    cst[:, 2, :] = (idx[None, :] >= idx[:, None]).astype(np.float32)
    cst[:, 3, :] = 1.0
    c32 = np.zeros((128, 3, 128), np.float32)
    c32[:, 0, :] = (idx[None, :] >= idx[:, None]).astype(np.float32)
    c32[:, 1, :] = 1.0
    jj, hh = idx // 8, idx % 8
    c32[:, 2, :] = ((hh[:, None] == hh[None, :]) & (jj[:, None] < jj[None, :])).astype(np.float32)
    shared = {
        'w_in_e': f(inp['w_in_e'][0]), 'w_out_e': f(inp['w_out_e'][0]),
        'conv_w_in': f(inp['conv_w_in'][0]), 'conv_w_out': f(inp['conv_w_out'][0]),
        'xa_wq': f(inp['xa_wq']), 'xa_wkv': f(inp['xa_wkv']), 'xa_wo': f(inp['xa_wo']),
        'ffn_w_gu': f(inp['ffn_w_gu']), 'ffn_w_down': f(inp['ffn_w_down']),
        'cols': cols, 'rows': rows, 'bout': f(inp['conv_b_out'][0]).reshape(1, D), 'wsT': wsT, 'cst': cst, 'c32': c32,
    }
    return shared


LAST_PROG = None
ALL_STAGES = ('mix0', 'xa0', 'ffn0', 'mix1', 'xa1', 'ffn1')
_NC_CACHE = {}


def kernel(**inputs):
    x = np.asarray(inputs['x'], dtype=np.float32)
    mem = np.asarray(inputs['mem'], dtype=np.float32)
    shared = host_prep(inputs)
    nseq = x.shape[0] // NCORES
    key = (nseq, ALL_STAGES)
    if key not in _NC_CACHE:
        _NC_CACHE[key] = build_program(nseq, ALL_STAGES)
    nc = _NC_CACHE[key]
    in_maps = []
    for c in range(NCORES):
        m = dict(shared)
        m['x'] = np.ascontiguousarray(x[c * nseq:(c + 1) * nseq])
        m['mem'] = np.ascontiguousarray(mem[c * nseq:(c + 1) * nseq])
        in_maps.append(m)
    res = run_bass_kernel_spmd(nc, in_maps, core_ids=list(range(NCORES)))
    return np.concatenate([r['out'] for r in res.results], axis=0)
```
